# Optimizing a Trainium2 kernel written in Bass

```python
import jax, jax.numpy as jnp
from jax import lax
import numpy as np

D_MODEL = 2048
BATCH = 1
SEQ = 16384
DEPTH = 1
DEC_BATCH = 4
DEC_SEQ = 2048
PAST_LEN = 128

ATTN_WIDTH = D_MODEL // 2
GMLP_WIDTH = D_MODEL - ATTN_WIDTH
HEAD_DIM = 128
N_HEADS = ATTN_WIDTH // HEAD_DIM
N_KV_HEADS = max(1, N_HEADS // 4)
GQA_GROUP = N_HEADS // N_KV_HEADS
GMLP_GROUPS = 8
GMLP_GROUP_DIM = GMLP_WIDTH // GMLP_GROUPS
CHUNK = 128
BLOCK = 128
WINDOW = 128
ROT_DIM = HEAD_DIM // 4
ROPE_THETA = 500000.0
D_FF = ((8 * D_MODEL // 3 + 255) // 256) * 256
EPS = 1e-6
NEG_INF = -1e30
IN_COLS = 2 * GMLP_WIDTH + N_HEADS * HEAD_DIM + 2 * N_KV_HEADS * HEAD_DIM

kernel_name = "hymba_gmlp_swa_macaron_encoder"


def rmsnorm(x, g):
    xf = x.astype(jnp.float32)
    xf = xf * lax.rsqrt(jnp.mean(xf * xf, axis=-1, keepdims=True) + EPS)
    return xf.astype(x.dtype) * g


def swiglu(x, w_gate, w_up, w_down):
    return (jax.nn.silu(x @ w_gate) * (x @ w_up)) @ w_down


def partial_rope(x):
    S = x.shape[1]
    half = ROT_DIM // 2
    inv_freq = ROPE_THETA ** (-jnp.arange(0, ROT_DIM, 2, dtype=jnp.float32) / ROT_DIM)
    ang = jnp.arange(S, dtype=jnp.float32)[:, None] * inv_freq[None, :]
    cos = jnp.cos(ang)[None, :, None, :]
    sin = jnp.sin(ang)[None, :, None, :]
    xr = x[..., :ROT_DIM].astype(jnp.float32)
    x1, x2 = xr[..., :half], xr[..., half:]
    rot = jnp.concatenate([x1 * cos - x2 * sin, x2 * cos + x1 * sin], axis=-1)
    return jnp.concatenate([rot.astype(x.dtype), x[..., ROT_DIM:]], axis=-1)


def chunked_spatial_gating(u, v, v_gain, w_s, b_s):
    B, S, _ = u.shape
    nc = S // CHUNK
    v = rmsnorm(v, v_gain)
    vb = v.reshape(B, nc, CHUNK, GMLP_GROUPS, GMLP_GROUP_DIM)
    mixed = jnp.einsum('gpq,bnqgc->bnpgc', w_s, vb) + b_s.T[None, None, :, :, None]
    return u * mixed.reshape(B, S, GMLP_WIDTH)


def windowed_gqa(q, k, v, sink):
    B, S, _, _ = q.shape
    nb = S // BLOCK
    qb = q.reshape(B, nb, BLOCK, N_KV_HEADS, GQA_GROUP, HEAD_DIM)

    def neighbours(t):
        tp = jnp.pad(t, ((0, 0), (BLOCK, BLOCK), (0, 0), (0, 0)))
        tb = tp.reshape(B, nb + 2, BLOCK, N_KV_HEADS, HEAD_DIM)
        return jnp.concatenate([tb[:, :-2], tb[:, 1:-1], tb[:, 2:]], axis=2)

    kb = neighbours(k)
    vb = neighbours(v)
    scale = HEAD_DIM ** -0.5
    scores = jnp.einsum('bnqkgd,bnjkd->bnkgqj', qb, kb).astype(jnp.float32) * scale
    blk = jnp.arange(nb)[:, None] * BLOCK
    qpos = blk + jnp.arange(BLOCK)[None, :]
    kpos = blk - BLOCK + jnp.arange(3 * BLOCK)[None, :]
    rel = kpos[:, None, :] - qpos[:, :, None]
    valid = (jnp.abs(rel) <= WINDOW) & (kpos[:, None, :] >= 0) & (kpos[:, None, :] < S)
    scores = jnp.where(valid[None, :, None, None], scores, NEG_INF)
    sink_l = sink.astype(jnp.float32).reshape(N_KV_HEADS, GQA_GROUP)[None, None, :, :, None, None]
    sink_b = jnp.broadcast_to(sink_l, scores.shape[:-1] + (1,))
    probs = jax.nn.softmax(jnp.concatenate([scores, sink_b], axis=-1), axis=-1)[..., :-1]
    out = jnp.einsum('bnkgqj,bnjkd->bnqkgd', probs.astype(v.dtype), vb)
    return out.reshape(B, S, N_HEADS * HEAD_DIM)


def encoder_layer(x, ffn1_norm, ffn1_w_gate, ffn1_w_up, ffn1_w_down, mix_norm, w_in,
                  gmlp_v_norm, gmlp_w_s, gmlp_b_s, attn_sink, out_norm_gmlp, out_norm_attn,
                  w_out, ffn2_norm, ffn2_w_gate, ffn2_w_up, ffn2_w_down):
    B, S, _ = x.shape
    h = x + 0.5 * swiglu(rmsnorm(x, ffn1_norm), ffn1_w_gate, ffn1_w_up, ffn1_w_down)
    z = rmsnorm(h, mix_norm) @ w_in
    c1 = GMLP_WIDTH
    c2 = c1 + GMLP_WIDTH
    c3 = c2 + N_HEADS * HEAD_DIM
    c4 = c3 + N_KV_HEADS * HEAD_DIM
    u = jax.nn.gelu(z[..., :c1])
    vg = jax.nn.gelu(z[..., c1:c2])
    q = partial_rope(z[..., c2:c3].reshape(B, S, N_HEADS, HEAD_DIM))
    k = partial_rope(z[..., c3:c4].reshape(B, S, N_KV_HEADS, HEAD_DIM))
    va = z[..., c4:].reshape(B, S, N_KV_HEADS, HEAD_DIM)
    a_out = chunked_spatial_gating(u, vg, gmlp_v_norm, gmlp_w_s, gmlp_b_s)
    b_out = windowed_gqa(q, k, va, attn_sink)
    merged = jnp.concatenate([rmsnorm(a_out, out_norm_gmlp), rmsnorm(b_out, out_norm_attn)], axis=-1)
    h = h + merged @ w_out
    h = h + 0.5 * swiglu(rmsnorm(h, ffn2_norm), ffn2_w_gate, ffn2_w_up, ffn2_w_down)
    return h


def setup_inputs(seed: int = 0) -> dict:
    key = jax.random.key(seed)
    ks = jax.random.split(key, 24)
    f32 = jnp.float32

    def nrm(k, shape, scale):
        return jax.random.normal(k, shape, f32) * scale

    def gain(k, shape):
        return 1.0 + 0.02 * jax.random.normal(k, shape, f32)

    L = DEPTH
    return {
        "x_prompt": jax.random.normal(ks[0], (BATCH, SEQ, D_MODEL), f32),
        "x_sample": jax.random.normal(ks[1], (DEC_BATCH, DEC_SEQ, D_MODEL), f32),
        "ffn1_norm": gain(ks[2], (L, D_MODEL)),
        "ffn1_w_gate": nrm(ks[3], (L, D_MODEL, D_FF), D_MODEL ** -0.5),
        "ffn1_w_up": nrm(ks[4], (L, D_MODEL, D_FF), D_MODEL ** -0.5),
        "ffn1_w_down": nrm(ks[5], (L, D_FF, D_MODEL), D_FF ** -0.5),
        "mix_norm": gain(ks[6], (L, D_MODEL)),
        "w_in": nrm(ks[7], (L, D_MODEL, IN_COLS), D_MODEL ** -0.5),
        "gmlp_v_norm": gain(ks[8], (L, GMLP_WIDTH)),
        "gmlp_w_s": nrm(ks[9], (L, GMLP_GROUPS, CHUNK, CHUNK), CHUNK ** -0.5),
        "gmlp_b_s": nrm(ks[10], (L, GMLP_GROUPS, CHUNK), 0.02),
        "attn_sink": nrm(ks[11], (L, N_HEADS), 0.5),
        "out_norm_gmlp": gain(ks[12], (L, GMLP_WIDTH)),
        "out_norm_attn": gain(ks[13], (L, ATTN_WIDTH)),
        "w_out": nrm(ks[14], (L, GMLP_WIDTH + ATTN_WIDTH, D_MODEL), D_MODEL ** -0.5),
        "ffn2_norm": gain(ks[15], (L, D_MODEL)),
        "ffn2_w_gate": nrm(ks[16], (L, D_MODEL, D_FF), D_MODEL ** -0.5),
        "ffn2_w_up": nrm(ks[17], (L, D_MODEL, D_FF), D_MODEL ** -0.5),
        "ffn2_w_down": nrm(ks[18], (L, D_FF, D_MODEL), D_FF ** -0.5),
        "final_norm": gain(ks[19], (D_MODEL,)),
    }


def reference(x_prompt, x_sample, ffn1_norm, ffn1_w_gate, ffn1_w_up, ffn1_w_down, mix_norm, w_in,
              gmlp_v_norm, gmlp_w_s, gmlp_b_s, attn_sink, out_norm_gmlp, out_norm_attn, w_out,
              ffn2_norm, ffn2_w_gate, ffn2_w_up, ffn2_w_down, final_norm):
    def trunk(x):
        h = x
        for l in range(DEPTH):
            h = encoder_layer(h, ffn1_norm[l], ffn1_w_gate[l], ffn1_w_up[l], ffn1_w_down[l],
                              mix_norm[l], w_in[l], gmlp_v_norm[l], gmlp_w_s[l], gmlp_b_s[l],
                              attn_sink[l], out_norm_gmlp[l], out_norm_attn[l], w_out[l],
                              ffn2_norm[l], ffn2_w_gate[l], ffn2_w_up[l], ffn2_w_down[l])
        return rmsnorm(h, final_norm)

    y_prompt = trunk(x_prompt)
    y_sample = trunk(x_sample)
    return (y_prompt, y_sample)
```

```python
import os
import numpy as np
import concourse.bass as bass
import concourse.mybir as mybir
from concourse.bass_utils import run_bass_kernel_spmd

F32 = mybir.dt.float32
BF16 = mybir.dt.bfloat16
I32 = mybir.dt.int32
AF = mybir.ActivationFunctionType
ALU = mybir.AluOpType
AX = mybir.AxisListType

D = 2048
DFF = 5632
NFC = DFF // 128
NGRP = NFC // 4
INC = 3584
EPS = 1e-6
MASKV = -30000.0
ROPE_THETA = 500000.0


class Op:
    __slots__ = ("eng", "fn", "reads", "writes", "deps", "flag", "fidx", "dn", "waits", "gi")

    def __init__(self, eng, fn, reads, writes):
        self.eng = eng
        self.fn = fn
        self.reads = tuple(reads)
        self.writes = tuple(writes)
        self.deps = ()
        self.flag = False
        self.fidx = -1
        self.dn = -1
        self.waits = []


class Sched:
    COMPUTE = ("pe", "act", "dve")
    DMAQ = ("pool", "sp")
    NROT = 4
    KDMA = 8

    def __init__(self, dry=False):
        self.ops = []
        self.dry = dry

    def add(self, eng, fn, reads=(), writes=(), tag=None):
        if self.dry or (tag is not None and tag in KSKIP):
            return
        if eng in ("act", "dve"):
            writes = tuple(writes) + tuple(r for r in reads if r[0] in ("ps", "psB"))
        self.ops.append(Op(eng, fn, reads, writes))

    def analyze(self):
        last_w = {}
        readers = {}
        for gi, op in enumerate(self.ops):
            op.gi = gi
            deps = set()
            for r in op.reads:
                w = last_w.get(r)
                if w is not None:
                    deps.add(w)
            for wr in op.writes:
                w = last_w.get(wr)
                if w is not None:
                    deps.add(w)
                for rd in readers.get(wr, {}).values():
                    deps.add(rd)
            deps.discard(op)
            for r in op.reads:
                rk = op.eng if op.eng in self.COMPUTE else (op.eng, gi)
                readers.setdefault(r, {})[rk] = op
            for wr in op.writes:
                last_w[wr] = op
                readers[wr] = {}
            op.deps = [d for d in deps if not (d.eng == "pe" and op.eng == "pe")]
            for d in op.deps:
                d.flag = True
        fcnt = {e: 0 for e in self.COMPUTE}
        dcnt = {q: 0 for q in self.DMAQ}
        for op in self.ops:
            if op.eng in self.DMAQ:
                op.dn = dcnt[op.eng]
                dcnt[op.eng] += 1
            elif op.flag:
                op.fidx = fcnt[op.eng]
                fcnt[op.eng] += 1
        waited_c = {}
        waited_d = {}
        for op in self.ops:
            ws = []
            if op.eng in self.DMAQ and op.dn >= self.KDMA:
                key = (op.eng, op.eng, op.dn % self.KDMA)
                val = 16 * (op.dn // self.KDMA)
                if waited_d.get(key, 0) < val:
                    waited_d[key] = val
                    ws.append(("d", op.eng, op.dn % self.KDMA, val))
            for d in sorted(op.deps, key=lambda o: o.gi):
                if d.eng in self.DMAQ:
                    key = (op.eng, d.eng, d.dn % self.KDMA)
                    val = 16 * (d.dn // self.KDMA + 1)
                    if waited_d.get(key, 0) < val:
                        waited_d[key] = val
                        ws.append(("d", d.eng, d.dn % self.KDMA, val))
                else:
                    key = (op.eng, d.eng)
                    if waited_c.get(key, -1) < d.fidx:
                        waited_c[key] = d.fidx
                        ws.append(("c", d.eng, d.fidx % self.NROT, d.fidx // self.NROT + 1))
            op.waits = ws
        self.stats = {"ops": len(self.ops), "flag": dict(fcnt), "dma": dict(dcnt)}

    def emit(self, nc, block, csem, dsem):
        per = {e: [] for e in self.COMPUTE + self.DMAQ}
        for op in self.ops:
            per[op.eng].append(op)

        def run(eng_name, e):
            for op in per[eng_name]:
                for w in op.waits:
                    if w[0] == "d":
                        e.wait_ge(dsem[w[1]][w[2]], w[3])
                    else:
                        e.wait_ge(csem[w[1]][w[2]], w[3])
                ins = op.fn(e)
                if op.eng in self.DMAQ:
                    ins.then_inc(dsem[op.eng][op.dn % self.KDMA], 16)
                elif op.flag:
                    ins.then_inc(csem[op.eng][op.fidx % self.NROT], 1)

        @block.tensor
        def _(e):
            run("pe", e)

        @block.scalar
        def _(e):
            run("act", e)

        @block.vector
        def _(e):
            run("dve", e)

        @block.gpsimd
        def _(e):
            run("pool", e)
            n = len(per["pool"])
            for s in range(self.KDMA):
                cnt = len([1 for k in range(n) if k % self.KDMA == s])
                if cnt:
                    e.wait_ge(dsem["pool"][s], 16 * cnt)

        @block.sync
        def _(e):
            run("sp", e)
            n = len(per["sp"])
            for s in range(self.KDMA):
                cnt = len([1 for k in range(n) if k % self.KDMA == s])
                if cnt:
                    e.wait_ge(dsem["sp"][s], 16 * cnt)


class Defer:
    def __init__(self):
        self.q = []
        self.step = 0

    def add(self, delay, fn):
        self.q.append((self.step + delay, fn))

    def tick(self):
        self.step += 1
        while True:
            due = [x for x in self.q if x[0] <= self.step]
            if not due:
                break
            self.q = [x for x in self.q if x[0] > self.step]
            for _, fn in due:
                fn()

    def flush(self):
        while self.q:
            self.tick()


class RR:
    def __init__(self, n):
        self.n = n
        self.p = 0

    def alloc(self, k=1):
        p = (self.p + k - 1) // k * k
        if p + k > self.n:
            p = 0
        self.p = p + k
        return list(range(p, p + k))


def make_cfg(n_prompt_own, n_sample_own):
    np_, ns_ = n_prompt_own, n_sample_own
    assert np_ >= 2
    order = ["pl", "pr", ("p", 0), ("p", 1), "sh"] + [("p", j) for j in range(2, np_)] + [("s", j) for j in range(ns_)]
    pos = {k: i for i, k in enumerate(order)}
    blocks = []
    for k in order:
        if isinstance(k, str):
            blocks.append(dict(kind="halo"))
        elif k[0] == "p":
            j = k[1]
            blocks.append(dict(kind="own", left=pos["pl"] if j == 0 else pos[("p", j - 1)],
                               right=pos["pr"] if j == np_ - 1 else pos[("p", j + 1)],
                               lm=2 if j == 0 else 0, rm=3 if j == np_ - 1 else 1))
        else:
            j = k[1]
            blocks.append(dict(kind="own", left=pos["sh"] if j == 0 else pos[("s", j - 1)],
                               right=pos["sh"] if j == ns_ - 1 else pos[("s", j + 1)],
                               lm=4 if j == 0 else 0, rm=5 if j == ns_ - 1 else 1))
    orow = 0
    for b in blocks:
        if b["kind"] == "own":
            b["orow"] = orow
            orow += 1
    nb = len(blocks)
    a_tiles = [list(range(s, min(s + 4, nb))) for s in range(0, nb, 4)]
    b_tiles = []
    done = set()
    for t in range(len(a_tiles)):
        last = a_tiles[t][-1]
        ready = [i for i in range(nb) if blocks[i]["kind"] == "own" and i not in done
                 and max(i, blocks[i]["left"], blocks[i]["right"]) <= last]
        take = ready if t == len(a_tiles) - 1 else (ready[:4] if len(ready) >= 4 else [])
        assert len(take) <= 4
        done.update(take)
        b_tiles.append(take)
    assert len(done) == orow
    return dict(blocks=blocks, a_tiles=a_tiles, b_tiles=b_tiles, n_own=orow, nb=nb)


KSKIP = set(os.environ.get('KSKIP', '').split(','))


class StopEmit(Exception):
    pass


def build_program(cfg, NH=6, NW=2, NQ=6, kstop=99):
    blocks = cfg["blocks"]
    NB = cfg["nb"]
    NOWN = cfg["n_own"]
    nc = bass.Bass("TRN2", target_bir_lowering=False)

    def din(name, shape, dt=F32):
        return nc.dram_tensor(name, list(shape), dt, kind="ExternalInput").ap()

    xs = din("xs", [NB, 128, D])
    rope_d = din("rope", [NB, 128, 64])
    masks_d = din("masks", [128, 6, 128])
    cst_d = din("cst", [128, 80])
    gbc_d = din("gbc", [128, 3072])
    ident_d = din("ident", [128, 128])
    wsT_d = din("wsT", [128, 8, 128])
    Wd = {}
    for L in (1, 2):
        Wd["g", L] = din(f"wg{L}", [D, DFF])
        Wd["u", L] = din(f"wu{L}", [D, DFF])
        Wd["d", L] = din(f"wd{L}", [DFF, D])
    w_in_d = din("w_in", [D, INC])
    w_out_d = din("w_out", [D, D])
    y_d = nc.dram_tensor("y", [NOWN * 128, D], F32, kind="ExternalOutput").ap()

    def dscr(name, shape):
        return nc.dram_tensor(name, list(shape), BF16, kind="Internal").ap()

    scr = {}
    for L in (1, 2):
        scr["gu", L] = dscr(f"sgu{L}", [NFC // 2, 128, 2, 16, 256])
        scr["d", L] = dscr(f"sd{L}", [NGRP, 128, 4, D])
    scr["in"] = dscr("sin", [7, 128, 16, 512])
    scr["out"] = dscr("sout", [4, 128, 4, D])

    steps = []
    for t in range(len(cfg["a_tiles"])):
        steps.append(("A", t))
        steps.append(("B", t))
    kv_last = {}
    for si, (k, t) in enumerate(steps):
        if k == "B":
            for b in cfg["b_tiles"][t]:
                for nb_ in (blocks[b]["left"], b, blocks[b]["right"]):
                    kv_last[nb_] = si
    live = 0
    mx = 0
    for si, (k, t) in enumerate(steps):
        if k == "A":
            live += len(cfg["a_tiles"][t])
            mx = max(mx, live)
        else:
            live -= len([b for b in kv_last if kv_last[b] == si])
    NKV = mx

    from contextlib import ExitStack
    es = ExitStack()

    def sb(name, shape, dt):
        return es.enter_context(nc.sbuf_tensor("sb_" + name, list(shape), dt))

    hbuf = sb("hbuf", [128, NH, D], F32)
    wring = sb("wring", [128, NW, 8192], BF16)
    xnT = sb("xnT", [128, 16, 512], BF16)
    actT = sb("actT", [128, 2, 4, 512], BF16)
    sgt = sb("sgt", [128, 2, 512], F32)
    kT = sb("kT", [128, NKV, 2, 128], BF16)
    vv = sb("vv", [128, NKV, 256], BF16)
    qT = sb("qT", [128, NQ, 8, 128], BF16)
    u_bf = sb("u_bf", [128, 4, 1024], BF16)
    vg_bf = sb("vg_bf", [128, 4, 1024], BF16)
    mT = sb("mT", [128, 8, 512], BF16)
    xn_bf = sb("xn_bf", [128, 2, D], BF16)
    a_bf = sb("a_bf", [128, 2, 1024], BF16)
    junk = sb("junk", [128, 1024], BF16)
    rp = sb("rp", [128, 2, 32, 4], F32)
    qb = sb("qb", [128, 2, 512], BF16)
    PT = sb("PT", [128, 2, 3, 512], BF16)
    den = sb("den", [128, 512], F32)
    bo = sb("bo", [128, 512], BF16)
    sq = sb("sq", [128, 512], BF16)
    gbc = sb("gbc", [128, 3072], F32)
    cst = sb("cst", [128, 80], F32)
    ropeT = sb("ropeT", [128, 4, 64], F32)
    masks = sb("masks", [128, 6, 4, 128], BF16)
    ident = sb("ident", [128, 128], BF16)
    ones = sb("ones", [128, 128], BF16)
    wsT = sb("wsT", [128, 8, 128], BF16)
    esink = sb("esink", [128, 8], F32)
    esr = sb("esr", [128, 8, 128], BF16)
    stat = sb("stat", [128, 64], F32)
    psF = es.enter_context(nc.psum_tensor("psF", [128, 6, 512], F32))
    psB = es.enter_context(nc.psum_tensor("psB", [128, 2, 1024], BF16))

    csem = {e: [es.enter_context(nc.semaphore(f"c_{e}{i}")) for i in range(Sched.NROT)] for e in Sched.COMPUTE}
    dsem = {q: [es.enter_context(nc.semaphore(f"d_{q}{i}")) for i in range(Sched.KDMA)] for q in Sched.DMAQ}

    C_G1, C_GM, C_G2, C_GOA, C_GOB, C_BS, C_SINK = 0, 16, 32, 48, 56, 64, 72
    ST_SS, ST_T, ST_R = 0, 1, 2
    ST_RA, ST_RB = 8, 16
    ST_A, ST_TT = 40, 44
    ST_N = 24

    def stage(n):
        if n > kstop:
            raise StopEmit()

    def emit_all(S, piece_seq, record):
        try:
            emit_all_(S, piece_seq, record)
        except StopEmit:
            pass

    def emit_all_(S, piece_seq, record):
        PB = RR(5)
        wstate = dict(i=0, loaded=0, uses={})

        def src_aps(key):
            kind = key[0]
            if kind == "gu":
                _, L, j = key
                return [(Wd[nm, L].rearrange("(k p) f -> p k f", p=128)[:, :, j * 256:(j + 1) * 256], gi * 4096, "p (k f) -> p k f", dict(k=16))
                        for gi, nm in enumerate(("g", "u"))]
            if kind == "d":
                _, L, g = key
                return [(Wd["d", L][g * 512:(g + 1) * 512, :].rearrange("(c p) d -> p c d", p=128), 0, "p (c d) -> p c d", dict(c=4))]
            if kind == "in":
                c = key[1]
                return [(w_in_d.rearrange("(k p) f -> p k f", p=128)[:, :, c * 512:(c + 1) * 512], 0, "p (k f) -> p k f", dict(k=16))]
            i = key[1]
            return [(w_out_d[i * 512:(i + 1) * 512, :].rearrange("(c p) d -> p c d", p=128), 0, "p (c d) -> p c d", dict(c=4))]

        def scr_ap(key):
            kind = key[0]
            if kind == "gu":
                return scr["gu", key[1]][key[2]].rearrange("p a k f -> p (a k f)")
            if kind == "d":
                return scr["d", key[1]][key[2]].rearrange("p c d -> p (c d)")
            if kind == "in":
                return scr["in"][key[1]].rearrange("p k f -> p (k f)")
            return scr["out"][key[1]].rearrange("p c d -> p (c d)")

        def wuse(key):
            i = wstate["i"]
            wstate["i"] += 1
            if record:
                piece_seq.append(key)
                return i % NW
            assert piece_seq[i] == key, (i, piece_seq[i], key)
            while wstate["loaded"] < min(len(piece_seq), i + NW):
                li = wstate["loaded"]
                k2 = piece_seq[li]
                slot = li % NW
                nuse = wstate["uses"].get(k2, 0)
                wstate["uses"][k2] = nuse + 1
                n_direct = 2 if k2[0] == "d" else 1
                if nuse < n_direct:
                    parts = src_aps(k2)
                    n = 8192 // len(parts)
                    for pi_, (src, off, pat, kw) in enumerate(parts):
                        dst = wring[:, slot, off:off + n].rearrange(pat, **kw)
                        wat = (("w", slot, pi_),) if len(parts) == 2 else (("w", slot, 0), ("w", slot, 1))
                        S.add("pool", lambda e, s=src, d=dst: e.dma_start(out=d, in_=s), writes=wat)
                    if nuse == n_direct - 1:
                        S.add("sp", lambda e, d=scr_ap(k2), sl=slot: e.dma_start(out=d, in_=wring[:, sl, :]),
                              reads=(("w", slot, 0), ("w", slot, 1)), writes=(("scr", k2),))
                else:
                    S.add("sp", lambda e, s=scr_ap(k2), sl=slot: e.dma_start(out=wring[:, sl, :], in_=s),
                          reads=(("scr", k2),), writes=(("w", slot, 0), ("w", slot, 1)))
                wstate["loaded"] += 1
            return i % NW

        def rstd_from_ss(ss_ap, nfeat, out_ap, ss_atoms, out_atom, n=1, niter=2):
            if not isinstance(ss_atoms[0], tuple):
                ss_atoms = (ss_atoms,)
            av = stat[:, ST_A:ST_A + n]
            tv = stat[:, ST_TT:ST_TT + n]
            A_, T_ = ("st", "rs_a"), ("st", "rs_t")
            S.add("dve", lambda e: e.tensor_scalar(out=av, in0=ss_ap, scalar1=1.0 / nfeat, scalar2=EPS,
                                                   op0=ALU.mult, op1=ALU.add), reads=tuple(ss_atoms), writes=(A_,))
            S.add("dve", lambda e: e.tensor_copy(out=tv, in_=av.bitcast(I32)), reads=(A_,), writes=(T_,))
            S.add("dve", lambda e: e.tensor_scalar(out=out_ap.bitcast(I32), in0=tv, scalar1=-0.5, scalar2=float(0x5f3759df),
                                                   op0=ALU.mult, op1=ALU.add), reads=(T_,), writes=(out_atom,))
            for _ in range(niter):
                if n == 1:
                    S.add("dve", lambda e: e.scalar_tensor_tensor(out=tv, in0=out_ap, scalar=av, in1=out_ap,
                                                                  op0=ALU.mult, op1=ALU.mult),
                          reads=(out_atom, A_), writes=(T_,))
                else:
                    S.add("dve", lambda e: e.tensor_tensor(out=tv, in0=out_ap, in1=out_ap, op=ALU.mult), reads=(out_atom,), writes=(T_,))
                    S.add("dve", lambda e: e.tensor_tensor(out=tv, in0=tv, in1=av, op=ALU.mult), reads=(T_, A_), writes=(T_,))
                S.add("dve", lambda e: e.tensor_scalar(out=tv, in0=tv, scalar1=-0.5, scalar2=1.5, op0=ALU.mult, op1=ALU.add),
                      reads=(T_,), writes=(T_,))
                S.add("dve", lambda e: e.tensor_tensor(out=out_ap, in0=out_ap, in1=tv, op=ALU.mult), reads=(out_atom, T_), writes=(out_atom,))

        def sumsq(src_ap, src_atoms, junk_ap, junk_atom, ss_ap, ss_atom):
            S.add("act", lambda e: e.activation(out=junk_ap, in_=src_ap, func=AF.Square, accum_out=ss_ap),
                  reads=src_atoms, writes=(junk_atom, ss_atom))

        def norm_stats(j, hs):
            sumsq(hbuf[:, hs, :], (("h", hs),), xn_bf[:, j % 2, :], ("xnbf", j % 2),
                  stat[:, ST_N + j:ST_N + j + 1], ("st", "nss", j))

        def norm_tile(slots, gcol, stats_done=False):
            nbk = len(slots)
            if not stats_done:
                for j, hs in enumerate(slots):
                    norm_stats(j, hs)
            rstd_from_ss(stat[:, ST_N:ST_N + nbk], D, stat[:, ST_N + 8:ST_N + 8 + nbk],
                         tuple(("st", "nss", j) for j in range(nbk)), ("st", "nr"), n=nbk)

            def stA(j, hs):
                hap = hbuf[:, hs, :]
                xb = j % 2
                rr = stat[:, ST_N + 8 + j:ST_N + 9 + j]
                S.add("act", lambda e: e.activation(out=xn_bf[:, xb, :], in_=hap, func=AF.Copy, scale=rr),
                      reads=(("h", hs), ("st", "nr")), writes=(("xnbf", xb),))

            def stB(j):
                xb = j % 2
                xa = ("xnbf", xb)
                for half in range(2):
                    for c in range(8):
                        k = half * 8 + c
                        S.add("pe", lambda e, half=half, c=c, k=k, xb=xb: e.transpose(
                            out=psB[:, half, c * 128:(c + 1) * 128], in_=xn_bf[:, xb, k * 128:(k + 1) * 128], identity=ident[:, :]),
                            reads=(xa, ("ident",)), writes=(("psB", half),))
                    gsrc = cst[:, gcol + half * 8: gcol + half * 8 + 8].unsqueeze(2).to_broadcast([128, 8, 128])
                    S.add("dve", lambda e, half=half, gsrc=gsrc, j=j: e.tensor_tensor(
                        out=xnT[:, half * 8:(half + 1) * 8, j * 128:(j + 1) * 128],
                        in0=psB[:, half, :].rearrange("p (c t) -> p c t", c=8), in1=gsrc, op=ALU.mult),
                        reads=(("psB", half), ("cst",)), writes=(("xnT", j),))

            for j in range(nbk + 1):
                if j < nbk:
                    stA(j, slots[j])
                if j >= 1:
                    stB(j - 1)

        def proj_tm(slot, lhs_fn, nk, hs, scale, extra_reads):
            wv = wring[:, slot, :].rearrange("p (c d) -> p c d", c=4)
            for half in range(2):
                banks = PB.alloc(2)
                for c in range(nk):
                    for q in range(2):
                        S.add("pe", lambda e, c=c, q=q, half=half, b=banks[q]: e.matmul(
                            psF[:, b, :], lhsT=lhs_fn(c), rhs=wv[:, c, half * 1024 + q * 512: half * 1024 + (q + 1) * 512],
                            start=(c == 0), stop=(c == nk - 1)),
                            reads=(("w", slot, 0), ("w", slot, 1)) + extra_reads, writes=(("ps", banks[q]),))
                pin = psF[:, banks[0]:banks[0] + 2, :].rearrange("p b n -> p (b n)")
                hap = hbuf[:, hs, half * 1024:(half + 1) * 1024]
                sc_reads = () if isinstance(scale, float) else (scale[1],)
                sc = scale if isinstance(scale, float) else scale[0]
                S.add("dve", lambda e, pin=pin, hap=hap, sc=sc: e.scalar_tensor_tensor(
                    out=hap, in0=pin, scalar=sc, in1=hap, op0=ALU.mult, op1=ALU.add),
                    reads=(("ps", banks[0]), ("ps", banks[1]), ("h", hs)) + sc_reads, writes=(("h", hs),))

        def ffn(L, tile_slots, last_hook=None):
            nbk = len(tile_slots)
            T = nbk * 128
            xatoms = tuple(("xnT", j) for j in range(nbk))

            def GU(g):
                buf = g % 2
                for pi in range(2):
                    slot = wuse(("gu", L, 2 * g + pi))
                    wv = wring[:, slot, :].rearrange("p (a k f) -> p a k f", a=2, k=16)
                    for cc in range(2):
                        c4 = pi * 2 + cc
                        banks = PB.alloc(2)
                        for a in range(2):
                            for k in range(16):
                                S.add("pe", lambda e, a=a, k=k, cc=cc, b=banks[a], wv=wv: e.matmul(
                                    psF[:, b, :T], lhsT=wv[:, a, k, cc * 128:(cc + 1) * 128], rhs=xnT[:, k, :T],
                                    start=(k == 0), stop=(k == 15)),
                                    reads=(("w", slot, 0), ("w", slot, 1)) + xatoms, writes=(("ps", banks[a]),))
                        si = c4 % 2
                        S.add("act", lambda e, b=banks[0], si=si: e.activation(out=sgt[:, si, :T], in_=psF[:, b, :T], func=AF.Silu),
                              reads=(("ps", banks[0]),), writes=(("sg", si),))
                        S.add("dve", lambda e, b=banks[1], si=si, buf=buf, c4=c4: e.tensor_tensor(
                            out=actT[:, buf, c4, :T], in0=psF[:, b, :T], in1=sgt[:, si, :T], op=ALU.mult),
                            reads=(("ps", banks[1]), ("sg", si)), writes=(("act", buf, c4),))

            def DN(g):
                buf = g % 2
                slot = wuse(("d", L, g))
                for j, hs in enumerate(tile_slots):
                    proj_tm(slot, lambda c, j=j, buf=buf: actT[:, buf, c, j * 128:(j + 1) * 128], 4, hs, 0.5,
                            tuple(("act", buf, c) for c in range(4)))
                    if g == NGRP - 1 and last_hook is not None:
                        last_hook(j, hs)

            for g in range(NGRP + 1):
                if g < NGRP:
                    GU(g)
                if g >= 1:
                    DN(g - 1)

        S.add("pool", lambda e: e.dma_start(out=cst[:, :], in_=cst_d), writes=(("cst",),))
        S.add("pool", lambda e: e.dma_start(out=gbc[:, :], in_=gbc_d), writes=(("gbc",),))
        for hh in range(4):
            S.add("pool", lambda e, hh=hh: e.dma_start(out=masks[:, :, hh, :], in_=masks_d), writes=(("masks",),))
        S.add("pool", lambda e: e.dma_start(out=ident[:, :], in_=ident_d), writes=(("ident",),))
        S.add("pool", lambda e: e.dma_start(out=wsT[:, :, :], in_=wsT_d), writes=(("wsT",),))
        S.add("dve", lambda e: e.memset(ones[:, :], 1.0), writes=(("ones",),))
        S.add("act", lambda e: e.activation(out=esink[:, :], in_=cst[:, C_SINK:C_SINK + 8], func=AF.Exp),
              reads=(("cst",),), writes=(("esink",),))
        S.add("dve", lambda e: e.tensor_scalar(out=esr[:, :, :], in0=esink[:, :].unsqueeze(2).to_broadcast([128, 8, 128]),
                                               scalar1=1.0 / 128.0, scalar2=None, op0=ALU.mult),
              reads=(("esink",),), writes=(("esr",),))

        hfree = list(range(NH))
        qfree = list(range(NQ))
        kvfree = list(range(NKV))
        hslot, qslot, kvslot = {}, {}, {}

        def phaseA(t):
            tb = cfg["a_tiles"][t]
            nbk = len(tb)
            for j, b in enumerate(tb):
                hs_ = hslot[b] = hfree.pop(0)
                kvslot[b] = kvfree.pop(0)
                if blocks[b]["kind"] == "own":
                    qslot[b] = qfree.pop(0)
                S.add("sp", lambda e, b=b, hs_=hs_: e.dma_start(out=hbuf[:, hs_, :], in_=xs[b]), writes=(("h", hs_),))
                S.add("pool", lambda e, b=b, j=j: e.dma_start(out=ropeT[:, j, :], in_=rope_d[b]), writes=(("rope", j),))
            slots = [hslot[b] for b in tb]
            stage(2)
            norm_tile(slots, C_G1)
            stage(3)
            ffn(1, slots, last_hook=norm_stats)
            stage(4)
            norm_tile(slots, C_GM, stats_done=True)
            stage(4.1)
            dq = Defer()

            def g_start(j):
                ss = stat[:, ST_SS:ST_SS + 1]
                rr = stat[:, ST_R:ST_R + 1]
                vga = (("vg", j, 0), ("vg", j, 1))
                sumsq(vg_bf[:, j, :], vga, junk[:, :], ("junk",), ss, ("st", "ss"))
                rstd_from_ss(ss, 1024, rr, ("st", "ss"), ("st", "r"))
                S.add("dve", lambda e, j=j: e.scalar_tensor_tensor(
                    out=vg_bf[:, j, :], in0=vg_bf[:, j, :], scalar=stat[:, ST_R:ST_R + 1], in1=gbc[:, 2048:3072],
                    op0=ALU.mult, op1=ALU.mult),
                    reads=vga + (("st", "r"), ("gbc",)), writes=vga)

            def g_mix(j):
                ss = stat[:, ST_SS:ST_SS + 1]
                vga = (("vg", j, 0), ("vg", j, 1))
                ua = (("u", j, 0), ("u", j, 1))
                ab = j % 2
                mb = PB.alloc(2)
                for g in range(8):
                    S.add("pe", lambda e, g=g, j=j, bk=mb[g // 4]: e.matmul(
                        psF[:, bk, (g % 4) * 128:(g % 4 + 1) * 128], lhsT=wsT[:, g, :],
                        rhs=vg_bf[:, j, g * 128:(g + 1) * 128], start=True, stop=True),
                        reads=vga + (("wsT",),), writes=(("ps", mb[g // 4]),))
                for g in range(8):
                    S.add("dve", lambda e, g=g, j=j, bk=mb[g // 4], ab=ab: e.scalar_tensor_tensor(
                        out=a_bf[:, ab, g * 128:(g + 1) * 128], in0=psF[:, bk, (g % 4) * 128:(g % 4 + 1) * 128],
                        scalar=cst[:, C_BS + g:C_BS + g + 1], in1=u_bf[:, j, g * 128:(g + 1) * 128],
                        op0=ALU.add, op1=ALU.mult),
                        reads=ua + (("ps", mb[g // 4]), ("cst",)), writes=(("abf", ab),))
                dq.add(1, lambda: g_stat(j))

            def g_stat(j):
                ss = stat[:, ST_SS:ST_SS + 1]
                ab = j % 2
                sumsq(a_bf[:, ab, :], (("abf", ab),), junk[:, :], ("junk",), ss, ("st", "ss"))
                rstd_from_ss(ss, 1024, stat[:, ST_RA + j:ST_RA + j + 1], ("st", "ss"), ("st", "ra", j))
                dq.add(1, lambda: g_tr(j))

            def g_tr(j):
                ab = j % 2
                for g in range(8):
                    S.add("pe", lambda e, g=g, ab=ab: e.transpose(
                        out=psB[:, 1, g * 128:(g + 1) * 128], in_=a_bf[:, ab, g * 128:(g + 1) * 128], identity=ident[:, :]),
                        reads=(("abf", ab), ("ident",)), writes=(("psB", 1),))
                gsrc = cst[:, C_GOA:C_GOA + 8].unsqueeze(2).to_broadcast([128, 8, 128])
                S.add("dve", lambda e, j=j, gsrc=gsrc: e.tensor_tensor(
                    out=mT[:, :, j * 128:(j + 1) * 128], in0=psB[:, 1, :].rearrange("p (c t) -> p c t", c=8),
                    in1=gsrc, op=ALU.mult),
                    reads=(("psB", 1), ("cst",)), writes=(("mT", j),))

            def q_tr(qi, nh, dst, watom):
                for h in range(nh):
                    S.add("pe", lambda e, h=h, qi=qi: e.transpose(
                        out=psB[:, 0, h * 128:(h + 1) * 128], in_=qb[:, qi, h * 128:(h + 1) * 128], identity=ident[:, :]),
                        reads=(("qb", qi, 0), ("qb", qi, 1), ("qb", qi, 2), ("ident",)), writes=(("psB", 0),))
                S.add("act", lambda e, dst=dst, nh=nh: e.activation(
                    out=dst, in_=psB[:, 0, 0:nh * 128].rearrange("p (h t) -> p h t", h=nh), func=AF.Copy),
                    reads=(("psB", 0),), writes=(watom,))

            ui = 0
            for c in (2, 3, 0, 1, 4, 5, 6):
                if c == 4:
                    stage(4.5)
                if c == 6:
                    stage(4.7)
                slot = wuse(("in", c))
                wv = wring[:, slot, :].rearrange("p (k f) -> p k f", k=16)
                for j, b in enumerate(tb):
                    own = blocks[b]["kind"] == "own"
                    if not own and c < 6:
                        continue
                    bank = PB.alloc(1)[0]
                    for k in range(16):
                        S.add("pe", lambda e, k=k, j=j, bank=bank, wv=wv: e.matmul(
                            psF[:, bank, :], lhsT=xnT[:, k, j * 128:(j + 1) * 128], rhs=wv[:, k, :],
                            start=(k == 0), stop=(k == 15)),
                            reads=(("w", slot, 0), ("w", slot, 1), ("xnT", j)), writes=(("ps", bank),))
                    dq.tick()
                    ui += 1
                    zp = psF[:, bank, :]
                    if c < 2:
                        S.add("act", lambda e, zp=zp, j=j, c=c: e.activation(
                            out=u_bf[:, j, c * 512:(c + 1) * 512], in_=zp, func=AF.Gelu_apprx_tanh),
                            reads=(("ps", bank),), writes=(("u", j, c),))
                        if c == 1 and own and kstop >= 4.3:
                            dq.add(1, lambda j=j: g_mix(j))
                    elif c < 4:
                        S.add("act", lambda e, zp=zp, j=j, c=c: e.activation(
                            out=vg_bf[:, j, (c - 2) * 512:(c - 1) * 512], in_=zp, func=AF.Gelu_apprx_tanh),
                            reads=(("ps", bank),), writes=(("vg", j, c - 2),))
                        if c == 3 and own and kstop >= 4.3:
                            g_start(j)
                    else:
                        nh = 4 if c < 6 else 2
                        qi = ui % 2
                        z3 = zp.rearrange("p (h d) -> p h d", h=4)
                        qb3 = qb[:, qi, :].rearrange("p (h d) -> p h d", h=4)
                        z3t = zp.rearrange("p (h d) -> p d h", h=4)
                        qb3t = qb[:, qi, :].rearrange("p (h d) -> p d h", h=4)
                        cs1 = ropeT[:, j, 0:32].unsqueeze(2).to_broadcast([128, 32, nh])
                        cs2 = ropeT[:, j, 32:64].unsqueeze(2).to_broadcast([128, 32, nh])
                        S.add("dve", lambda e, z3t=z3t, cs1=cs1, nh=nh: e.tensor_tensor(
                            out=rp[:, 0, :, 0:nh], in0=z3t[:, 0:32, 0:nh], in1=cs1, op=ALU.mult),
                            reads=(("ps", bank), ("rope", j)), writes=(("rp", 0),))
                        S.add("dve", lambda e, z3t=z3t, cs2=cs2, nh=nh: e.tensor_tensor(
                            out=rp[:, 1, :, 0:nh], in0=z3t[:, 0:32, 0:nh], in1=cs2, op=ALU.mult),
                            reads=(("ps", bank), ("rope", j)), writes=(("rp", 1),))
                        S.add("act", lambda e, z3=z3, qb3=qb3, nh=nh: e.activation(
                            out=qb3[:, 0:nh, 32:128], in_=z3[:, 0:nh, 32:128], func=AF.Copy),
                            reads=(("ps", bank),), writes=(("qb", qi, 2),))
                        if c == 6:
                            S.add("act", lambda e, zp=zp, ks_=kvslot[b]: e.activation(
                                out=vv[:, ks_, :], in_=zp[:, 256:512], func=AF.Copy),
                                reads=(("ps", bank),), writes=(("kv", kvslot[b], "v"),))
                        S.add("dve", lambda e, qb3t=qb3t, nh=nh: e.tensor_tensor(
                            out=qb3t[:, 0:16, 0:nh], in0=rp[:, 0, 0:16, 0:nh], in1=rp[:, 0, 16:32, 0:nh], op=ALU.subtract),
                            reads=(("rp", 0),), writes=(("qb", qi, 0),))
                        S.add("dve", lambda e, qb3t=qb3t, nh=nh: e.tensor_tensor(
                            out=qb3t[:, 16:32, 0:nh], in0=rp[:, 1, 0:16, 0:nh], in1=rp[:, 1, 16:32, 0:nh], op=ALU.add),
                            reads=(("rp", 1),), writes=(("qb", qi, 1),))
                        if c < 6:
                            dst = qT[:, qslot[b], (c - 4) * 4:(c - 3) * 4, :]
                            watom = ("q", qslot[b], c - 4)
                        else:
                            dst = kT[:, kvslot[b], :, :]
                            watom = ("kv", kvslot[b], "k")
                        dq.add(1, lambda qi=qi, nh=nh, dst=dst, watom=watom: q_tr(qi, nh, dst, watom))
            dq.flush()
            stage(5)
            for i in range(2):
                slot = wuse(("out", i))
                for j, b in enumerate(tb):
                    if blocks[b]["kind"] != "own":
                        continue
                    proj_tm(slot, lambda c, j=j, i=i: mT[:, i * 4 + c, j * 128:(j + 1) * 128], 4, hslot[b],
                            (stat[:, ST_RA + j:ST_RA + j + 1], ("st", "ra", j)), (("mT", j),))
            for b in tb:
                if blocks[b]["kind"] != "own":
                    hfree.append(hslot.pop(b))

        def phaseB(t, si):
            tb = cfg["b_tiles"][t]
            if not tb:
                return
            scale = 1.0 / np.sqrt(128.0)
            stage(6)
            dq = Defer()

            def att_P(j, kvh, nbrs, pb):
                bo_bank = PB.alloc(1)[0]
                bd_bank = PB.alloc(1)[0]
                for kb, (nb_, mid) in enumerate(nbrs):
                    ks = kvslot_snapshot[nb_]
                    S.add("pe", lambda e, ks=ks, kb=kb, kvh=kvh, bank=bo_bank, pb=pb: e.matmul(
                        psF[:, bank, :], lhsT=vv[:, ks, kvh * 128:(kvh + 1) * 128], rhs=PT[:, pb, kb, :],
                        start=(kb == 0), stop=(kb == 2)),
                        reads=(("kv", ks, "v"), ("PT", pb, kb)), writes=(("ps", bo_bank),))
                for kb in range(3):
                    S.add("pe", lambda e, kb=kb, bank=bd_bank, pb=pb: e.matmul(
                        psF[:, bank, :], lhsT=ones[:, :], rhs=PT[:, pb, kb, :], start=(kb == 0), stop=False),
                        reads=(("ones",), ("PT", pb, kb)), writes=(("ps", bd_bank),))
                S.add("pe", lambda e, bank=bd_bank, kvh=kvh: e.matmul(
                    psF[:, bank, :], lhsT=ones[:, :], rhs=esr[:, kvh * 4:(kvh + 1) * 4, :], start=False, stop=True),
                    reads=(("ones",), ("esr",)), writes=(("ps", bd_bank),))
                S.add("act", lambda e, bank=bd_bank: e.activation(out=den[:, :], in_=psF[:, bank, :], func=AF.Ln),
                      reads=(("ps", bd_bank),), writes=(("den",),))
                S.add("act", lambda e: e.activation(out=den[:, :], in_=den[:, :], func=AF.Exp, scale=-1.0),
                      reads=(("den",),), writes=(("den",),))
                S.add("dve", lambda e, bank=bo_bank: e.tensor_tensor(out=bo[:, :], in0=psF[:, bank, :], in1=den[:, :], op=ALU.mult),
                      reads=(("ps", bo_bank), ("den",)), writes=(("bo",),))
                S.add("dve", lambda e: e.tensor_tensor(out=sq[:, :], in0=bo[:, :], in1=bo[:, :], op=ALU.mult),
                      reads=(("bo",),), writes=(("sq",),))
                gob = cst[:, C_GOB + kvh * 4:C_GOB + kvh * 4 + 4].unsqueeze(2).to_broadcast([128, 4, 128])
                S.add("dve", lambda e, j=j, kvh=kvh, gob=gob: e.tensor_tensor(
                    out=mT[:, kvh * 4:(kvh + 1) * 4, j * 128:(j + 1) * 128],
                    in0=bo[:, :].rearrange("p (h t) -> p h t", h=4), in1=gob, op=ALU.mult),
                    reads=(("bo",), ("cst",)), writes=(("mT", j),))
                dq.add(1, lambda: att_Q(j, kvh))

            def att_Q(j, kvh):
                for h in range(4):
                    S.add("pe", lambda e, h=h, kvh=kvh: e.matmul(
                        psF[:, 5, 0:1], lhsT=sq[:, h * 128:(h + 1) * 128], rhs=ones[:, 0:1],
                        start=(kvh == 0 and h == 0), stop=(kvh == 1 and h == 3)),
                        reads=(("sq",), ("ones",)), writes=(("ps", 5),))
                if kvh == 1:
                    ssb = stat[:, ST_SS:ST_SS + 1]
                    S.add("act", lambda e: e.activation(out=stat[:, ST_SS:ST_SS + 1], in_=psF[:, 5, 0:1], func=AF.Copy),
                          reads=(("ps", 5),), writes=(("st", "ss"),))
                    rstd_from_ss(ssb, 1024, stat[:, ST_RB + j:ST_RB + j + 1], ("st", "ss"), ("st", "rb", j))

            kvslot_snapshot = dict(kvslot)
            ui = 0
            for j, b in enumerate(tb):
                bl = blocks[b]
                nbrs = [(bl["left"], bl["lm"]), (b, None), (bl["right"], bl["rm"])]
                qs = qslot[b]
                for kvh in range(2):
                    pb = ui % 2
                    ui += 1
                    for kb, (nb_, mid) in enumerate(nbrs):
                        bank = PB.alloc(1)[0]
                        ks = kvslot[nb_]
                        S.add("pe", lambda e, bank=bank, ks=ks, kvh=kvh, qs=qs, mid=mid: e.matmul(
                            psF[:, bank, :], lhsT=kT[:, ks, kvh, :], rhs=qT[:, qs, kvh * 4:(kvh + 1) * 4, :],
                            start=True, stop=(mid is None)),
                            reads=(("kv", ks, "k"), ("q", qs, kvh)), writes=(("ps", bank),))
                        if mid is not None:
                            S.add("pe", lambda e, bank=bank, mid=mid: e.matmul(
                                psF[:, bank, :], lhsT=ident[:, :],
                                rhs=masks[:, mid, :, :],
                                start=False, stop=True),
                                reads=(("ident",), ("masks",)), writes=(("ps", bank),))
                        S.add("act", lambda e, bank=bank, kb=kb, pb=pb: e.activation(
                            out=PT[:, pb, kb, :], in_=psF[:, bank, :], func=AF.Exp, scale=float(scale)),
                            reads=(("ps", bank),), writes=(("PT", pb, kb),))
                    dq.tick()
                    dq.add(1, lambda j=j, kvh=kvh, nbrs=nbrs, pb=pb: att_P(j, kvh, nbrs, pb))
            dq.flush()
            stage(7)
            for i in range(2):
                slot = wuse(("out", 2 + i))
                for j, b in enumerate(tb):
                    proj_tm(slot, lambda c, j=j, i=i: mT[:, i * 4 + c, j * 128:(j + 1) * 128], 4, hslot[b],
                            (stat[:, ST_RB + j:ST_RB + j + 1], ("st", "rb", j)), (("mT", j),))
                    if i == 1:
                        norm_stats(j, hslot[b])
            stage(8)
            norm_tile([hslot[b] for b in tb], C_G2, stats_done=True)
            ffn(2, [hslot[b] for b in tb], last_hook=norm_stats)
            stage(9)
            rstd_from_ss(stat[:, ST_N:ST_N + len(tb)], D, stat[:, ST_N + 8:ST_N + 8 + len(tb)],
                         tuple(("st", "nss", j) for j in range(len(tb))), ("st", "nr"), n=len(tb), niter=3)
            for j, b in enumerate(tb):
                hs = hslot[b]
                hap = hbuf[:, hs, :]
                rr = stat[:, ST_N + 8 + j:ST_N + 9 + j]
                S.add("dve", lambda e, hap=hap, rr=rr: e.scalar_tensor_tensor(
                    out=hap, in0=hap, scalar=rr, in1=gbc[:, 0:2048], op0=ALU.mult, op1=ALU.mult),
                    reads=(("h", hs), ("st", "nr"), ("gbc",)), writes=(("h", hs),))
                orow = blocks[b]["orow"]
                S.add("sp", lambda e, hap=hap, orow=orow: e.dma_start(out=y_d[orow * 128:(orow + 1) * 128, :], in_=hap),
                      reads=(("h", hs),), writes=(("y", orow),))
                hfree.append(hslot.pop(b))
                qfree.append(qslot.pop(b))
            for b_ in [b_ for b_ in list(kvslot) if kv_last.get(b_, -1) == si]:
                kvfree.append(kvslot.pop(b_))

        for si, (k, t) in enumerate(steps):
            if k == "A":
                phaseA(t)
            else:
                phaseB(t, si)

    piece_seq = []
    emit_all(Sched(dry=True), piece_seq, True)
    S = Sched()
    emit_all(S, piece_seq, False)
    S.analyze()
    with nc.Block() as block:
        S.emit(nc, block, csem, dsem)
    es.close()
    return nc, S.stats


def rope_table(pos):
    inv = (np.float32(ROPE_THETA) ** (-np.arange(0, 32, 2, dtype=np.float32) / np.float32(32))).astype(np.float32)
    ang = pos.astype(np.float32)[:, None] * inv[None, :]
    c = np.cos(ang).astype(np.float32)
    s = np.sin(ang).astype(np.float32)
    return np.concatenate([c, s, s, c], axis=1)


def band_masks(l_p1, r_pl, l_s0, r_sl):
    j = np.arange(128)[:, None]
    i = np.arange(128)[None, :]
    Lm = np.where(j >= i, 0.0, MASKV).astype(np.float32)
    Rm = np.where(j <= i, 0.0, MASKV).astype(np.float32)
    full = np.full((128, 128), MASKV, np.float32)
    m = np.stack([Lm, Rm, Lm if l_p1 else full, Rm if r_pl else full, Lm if l_s0 else full, Rm if r_sl else full], axis=1)
    return np.ascontiguousarray(m)


def common_inputs(ffn1_norm, ffn1_w_gate, ffn1_w_up, ffn1_w_down, mix_norm, w_in, gmlp_v_norm, gmlp_w_s, gmlp_b_s,
                  attn_sink, out_norm_gmlp, out_norm_attn, w_out, ffn2_norm, ffn2_w_gate, ffn2_w_up, ffn2_w_down,
                  final_norm):
    f = lambda a: np.ascontiguousarray(np.asarray(a, dtype=np.float32))
    cst = np.zeros((128, 80), np.float32)
    cst[:, 0:16] = f(ffn1_norm)[0].reshape(16, 128).T
    cst[:, 16:32] = f(mix_norm)[0].reshape(16, 128).T
    cst[:, 32:48] = f(ffn2_norm)[0].reshape(16, 128).T
    cst[:, 48:56] = f(out_norm_gmlp)[0].reshape(8, 128).T
    cst[:, 56:64] = f(out_norm_attn)[0].reshape(8, 128).T
    cst[:, 64:72] = f(gmlp_b_s)[0].T
    cst[:, 72:80] = np.broadcast_to(f(attn_sink)[0][None, :], (128, 8))
    gbc = np.empty((128, 3072), np.float32)
    gbc[:, 0:2048] = np.broadcast_to(f(final_norm)[None, :], (128, 2048))
    gbc[:, 2048:3072] = np.broadcast_to(f(gmlp_v_norm)[0][None, :], (128, 1024))
    wsT = np.ascontiguousarray(np.transpose(f(gmlp_w_s)[0], (2, 0, 1)))
    return {
        "cst": cst, "gbc": gbc, "ident": np.eye(128, dtype=np.float32), "wsT": wsT,
        "wg1": f(ffn1_w_gate)[0], "wu1": f(ffn1_w_up)[0], "wd1": f(ffn1_w_down)[0],
        "wg2": f(ffn2_w_gate)[0], "wu2": f(ffn2_w_up)[0], "wd2": f(ffn2_w_down)[0],
        "w_in": f(w_in)[0], "w_out": f(w_out)[0],
    }


_CACHE = {}


def kernel(x_prompt, x_sample, **w):
    xp = np.asarray(x_prompt, dtype=np.float32)
    xsm = np.asarray(x_sample, dtype=np.float32)
    NPO, NSO = 16, 8
    cfg = make_cfg(NPO, NSO)
    if "nc" not in _CACHE:
        _CACHE["nc"] = build_program(cfg)[0]
    nc = _CACHE["nc"]
    com = common_inputs(**w)
    SEQ = xp.shape[1]
    npb = SEQ // 128
    in_maps = []
    zero = np.zeros((128, D), np.float32)
    for i in range(8):
        blks, ropes = [], []

        def addp(gb):
            if 0 <= gb < npb:
                blks.append(xp[0, gb * 128:(gb + 1) * 128])
            else:
                blks.append(zero)
            ropes.append(rope_table(np.arange(gb * 128, (gb + 1) * 128)))
        sb_, hf = i // 2, i % 2
        halo = NSO if hf == 0 else NSO - 1
        addp(NPO * i - 1)
        addp(NPO * i + NPO)
        addp(NPO * i)
        addp(NPO * i + 1)
        blks.append(xsm[sb_, halo * 128:(halo + 1) * 128])
        ropes.append(rope_table(np.arange(halo * 128, (halo + 1) * 128)))
        for j in range(2, NPO):
            addp(NPO * i + j)
        for sblk in [NSO * hf + j for j in range(NSO)]:
            blks.append(xsm[sb_, sblk * 128:(sblk + 1) * 128])
            ropes.append(rope_table(np.arange(sblk * 128, (sblk + 1) * 128)))
        m = dict(com)
        m["xs"] = np.ascontiguousarray(np.stack(blks, 0))
        m["rope"] = np.ascontiguousarray(np.stack(ropes, 0))
        m["masks"] = band_masks(l_p1=(i > 0), r_pl=(i < 7), l_s0=(hf == 1), r_sl=(hf == 0))
        in_maps.append(m)
    res = run_bass_kernel_spmd(nc, in_maps, core_ids=list(range(8)))
    y_p = np.empty_like(xp)
    y_s = np.empty_like(xsm)
    for i in range(8):
        y = res.results[i]["y"]
        y_p[0, 2048 * i:2048 * (i + 1)] = y[:2048]
        y_s[i // 2, 1024 * (i % 2):1024 * (i % 2 + 1)] = y[2048:3072]
    return (y_p, y_s)
```

```python
import os
import numpy as np
import concourse.bass as bass
import concourse.mybir as mybir
from concourse.bass_utils import run_bass_kernel_spmd

F32 = mybir.dt.float32
BF16 = mybir.dt.bfloat16
I32 = mybir.dt.int32
AF = mybir.ActivationFunctionType
ALU = mybir.AluOpType
AX = mybir.AxisListType

D = 2048
DFF = 5632
NFC = DFF // 128
NGRP = NFC // 4
INC = 3584
EPS = 1e-6
MASKV = -30000.0
ROPE_THETA = 500000.0


class Op:
    __slots__ = ("eng", "fn", "reads", "writes", "deps", "flag", "fidx", "dn", "waits", "gi")

    def __init__(self, eng, fn, reads, writes):
        self.eng = eng
        self.fn = fn
        self.reads = tuple(reads)
        self.writes = tuple(writes)
        self.deps = ()
        self.flag = False
        self.fidx = -1
        self.dn = -1
        self.waits = []


class Sched:
    COMPUTE = ("pe", "act", "dve")
    DMAQ = ("pool", "sp")
    NROT = 4
    KDMA = 8

    def __init__(self, dry=False):
        self.ops = []
        self.dry = dry

    def add(self, eng, fn, reads=(), writes=(), tag=None):
        if self.dry or (tag is not None and tag in KSKIP):
            return
        if eng in ("act", "dve"):
            writes = tuple(writes) + tuple(r for r in reads if r[0] in ("ps", "psB"))
        self.ops.append(Op(eng, fn, reads, writes))

    def analyze(self):
        last_w = {}
        readers = {}
        for gi, op in enumerate(self.ops):
            op.gi = gi
            deps = set()
            for r in op.reads:
                w = last_w.get(r)
                if w is not None:
                    deps.add(w)
            for wr in op.writes:
                w = last_w.get(wr)
                if w is not None:
                    deps.add(w)
                for rd in readers.get(wr, {}).values():
                    deps.add(rd)
            deps.discard(op)
            for r in op.reads:
                rk = op.eng if op.eng in self.COMPUTE else (op.eng, gi)
                readers.setdefault(r, {})[rk] = op
            for wr in op.writes:
                last_w[wr] = op
                readers[wr] = {}
            op.deps = [d for d in deps if not (d.eng == "pe" and op.eng == "pe")]
            for d in op.deps:
                d.flag = True
        fcnt = {e: 0 for e in self.COMPUTE}
        dcnt = {q: 0 for q in self.DMAQ}
        for op in self.ops:
            if op.eng in self.DMAQ:
                op.dn = dcnt[op.eng]
                dcnt[op.eng] += 1
            elif op.flag:
                op.fidx = fcnt[op.eng]
                fcnt[op.eng] += 1
        waited_c = {}
        waited_d = {}
        for op in self.ops:
            ws = []
            if op.eng in self.DMAQ and op.dn >= self.KDMA:
                key = (op.eng, op.eng, op.dn % self.KDMA)
                val = 16 * (op.dn // self.KDMA)
                if waited_d.get(key, 0) < val:
                    waited_d[key] = val
                    ws.append(("d", op.eng, op.dn % self.KDMA, val))
            for d in sorted(op.deps, key=lambda o: o.gi):
                if d.eng in self.DMAQ:
                    key = (op.eng, d.eng, d.dn % self.KDMA)
                    val = 16 * (d.dn // self.KDMA + 1)
                    if waited_d.get(key, 0) < val:
                        waited_d[key] = val
                        ws.append(("d", d.eng, d.dn % self.KDMA, val))
                else:
                    key = (op.eng, d.eng)
                    if waited_c.get(key, -1) < d.fidx:
                        waited_c[key] = d.fidx
                        ws.append(("c", d.eng, d.fidx % self.NROT, d.fidx // self.NROT + 1))
            op.waits = ws
        self.stats = {"ops": len(self.ops), "flag": dict(fcnt), "dma": dict(dcnt)}

    def emit(self, nc, block, csem, dsem):
        per = {e: [] for e in self.COMPUTE + self.DMAQ}
        for op in self.ops:
            per[op.eng].append(op)

        def run(eng_name, e):
            for op in per[eng_name]:
                for w in op.waits:
                    if w[0] == "d":
                        e.wait_ge(dsem[w[1]][w[2]], w[3])
                    else:
                        e.wait_ge(csem[w[1]][w[2]], w[3])
                ins = op.fn(e)
                if op.eng in self.DMAQ:
                    ins.then_inc(dsem[op.eng][op.dn % self.KDMA], 16)
                elif op.flag:
                    ins.then_inc(csem[op.eng][op.fidx % self.NROT], 1)

        @block.tensor
        def _(e):
            run("pe", e)

        @block.scalar
        def _(e):
            run("act", e)

        @block.vector
        def _(e):
            run("dve", e)

        @block.gpsimd
        def _(e):
            run("pool", e)
            n = len(per["pool"])
            for s in range(self.KDMA):
                cnt = len([1 for k in range(n) if k % self.KDMA == s])
                if cnt:
                    e.wait_ge(dsem["pool"][s], 16 * cnt)

        @block.sync
        def _(e):
            run("sp", e)
            n = len(per["sp"])
            for s in range(self.KDMA):
                cnt = len([1 for k in range(n) if k % self.KDMA == s])
                if cnt:
                    e.wait_ge(dsem["sp"][s], 16 * cnt)


class Defer:
    def __init__(self):
        self.q = []
        self.step = 0

    def add(self, delay, fn):
        self.q.append((self.step + delay, fn))

    def tick(self):
        self.step += 1
        while True:
            due = [x for x in self.q if x[0] <= self.step]
            if not due:
                break
            self.q = [x for x in self.q if x[0] > self.step]
            for _, fn in due:
                fn()

    def flush(self):
        while self.q:
            self.tick()


class RR:
    def __init__(self, n):
        self.n = n
        self.p = 0

    def alloc(self, k=1):
        p = (self.p + k - 1) // k * k
        if p + k > self.n:
            p = 0
        self.p = p + k
        return list(range(p, p + k))


def make_cfg(n_prompt_own, n_sample_own):
    np_, ns_ = n_prompt_own, n_sample_own
    assert np_ >= 2
    order = ["pl", "pr", ("p", 0), ("p", 1), "sh"] + [("p", j) for j in range(2, np_)] + [("s", j) for j in range(ns_)]
    pos = {k: i for i, k in enumerate(order)}
    blocks = []
    for k in order:
        if isinstance(k, str):
            blocks.append(dict(kind="halo"))
        elif k[0] == "p":
            j = k[1]
            blocks.append(dict(kind="own", left=pos["pl"] if j == 0 else pos[("p", j - 1)],
                               right=pos["pr"] if j == np_ - 1 else pos[("p", j + 1)],
                               lm=2 if j == 0 else 0, rm=3 if j == np_ - 1 else 1))
        else:
            j = k[1]
            blocks.append(dict(kind="own", left=pos["sh"] if j == 0 else pos[("s", j - 1)],
                               right=pos["sh"] if j == ns_ - 1 else pos[("s", j + 1)],
                               lm=4 if j == 0 else 0, rm=5 if j == ns_ - 1 else 1))
    orow = 0
    for b in blocks:
        if b["kind"] == "own":
            b["orow"] = orow
            orow += 1
    nb = len(blocks)
    a_tiles = [list(range(s, min(s + 4, nb))) for s in range(0, nb, 4)]
    b_tiles = []
    done = set()
    for t in range(len(a_tiles)):
        last = a_tiles[t][-1]
        ready = [i for i in range(nb) if blocks[i]["kind"] == "own" and i not in done
                 and max(i, blocks[i]["left"], blocks[i]["right"]) <= last]
        take = ready if t == len(a_tiles) - 1 else (ready[:4] if len(ready) >= 4 else [])
        assert len(take) <= 4
        done.update(take)
        b_tiles.append(take)
    assert len(done) == orow
    return dict(blocks=blocks, a_tiles=a_tiles, b_tiles=b_tiles, n_own=orow, nb=nb)


KSKIP = set(os.environ.get('KSKIP', '').split(','))


class StopEmit(Exception):
    pass


def build_program(cfg, NH=6, NW=2, NQ=6, kstop=99):
    blocks = cfg["blocks"]
    NB = cfg["nb"]
    NOWN = cfg["n_own"]
    nc = bass.Bass("TRN2", target_bir_lowering=False)

    def din(name, shape, dt=F32):
        return nc.dram_tensor(name, list(shape), dt, kind="ExternalInput").ap()

    xs = din("xs", [NB, 128, D])
    rope_d = din("rope", [NB, 128, 64])
    masks_d = din("masks", [128, 6, 128])
    cst_d = din("cst", [128, 80])
    gbc_d = din("gbc", [128, 3072])
    ident_d = din("ident", [128, 128])
    wsT_d = din("wsT", [128, 8, 128])
    Wd = {}
    for L in (1, 2):
        Wd["g", L] = din(f"wg{L}", [D, DFF])
        Wd["u", L] = din(f"wu{L}", [D, DFF])
        Wd["d", L] = din(f"wd{L}", [DFF, D])
    w_in_d = din("w_in", [D, INC])
    w_out_d = din("w_out", [D, D])
    y_d = nc.dram_tensor("y", [NOWN * 128, D], F32, kind="ExternalOutput").ap()

    def dscr(name, shape):
        return nc.dram_tensor(name, list(shape), BF16, kind="Internal").ap()

    scr = {}
    for L in (1, 2):
        scr["gu", L] = dscr(f"sgu{L}", [NFC // 2, 128, 2, 16, 256])
        scr["d", L] = dscr(f"sd{L}", [NGRP, 128, 4, D])
    scr["in"] = dscr("sin", [7, 128, 16, 512])
    scr["out"] = dscr("sout", [4, 128, 4, D])

    steps = []
    for t in range(len(cfg["a_tiles"])):
        steps.append(("A", t))
        steps.append(("B", t))
    kv_last = {}
    for si, (k, t) in enumerate(steps):
        if k == "B":
            for b in cfg["b_tiles"][t]:
                for nb_ in (blocks[b]["left"], b, blocks[b]["right"]):
                    kv_last[nb_] = si
    live = 0
    mx = 0
    for si, (k, t) in enumerate(steps):
        if k == "A":
            live += len(cfg["a_tiles"][t])
            mx = max(mx, live)
        else:
            live -= len([b for b in kv_last if kv_last[b] == si])
    NKV = mx

    from contextlib import ExitStack
    es = ExitStack()

    def sb(name, shape, dt):
        return es.enter_context(nc.sbuf_tensor("sb_" + name, list(shape), dt))

    hbuf = sb("hbuf", [128, NH, D], F32)
    wring = sb("wring", [128, NW, 8192], BF16)
    xnT = sb("xnT", [128, 16, 512], BF16)
    actT = sb("actT", [128, 2, 4, 512], BF16)
    sgt = sb("sgt", [128, 2, 512], F32)
    kT = sb("kT", [128, NKV, 2, 128], BF16)
    vv = sb("vv", [128, NKV, 256], BF16)
    qT = sb("qT", [128, NQ, 8, 128], BF16)
    u_bf = sb("u_bf", [128, 4, 1024], BF16)
    vg_bf = sb("vg_bf", [128, 4, 1024], BF16)
    mT = sb("mT", [128, 8, 512], BF16)
    xn_bf = sb("xn_bf", [128, 2, D], BF16)
    a_bf = sb("a_bf", [128, 2, 1024], BF16)
    junk = sb("junk", [128, 1024], BF16)
    rp = sb("rp", [128, 2, 32, 4], F32)
    qb = sb("qb", [128, 2, 512], BF16)
    PT = sb("PT", [128, 2, 3, 512], BF16)
    den = sb("den", [128, 512], F32)
    bo = sb("bo", [128, 512], BF16)
    sq = sb("sq", [128, 512], BF16)
    gbc = sb("gbc", [128, 3072], F32)
    cst = sb("cst", [128, 80], F32)
    ropeT = sb("ropeT", [128, 4, 64], F32)
    masks = sb("masks", [128, 6, 4, 128], BF16)
    ident = sb("ident", [128, 128], BF16)
    ones = sb("ones", [128, 128], BF16)
    wsT = sb("wsT", [128, 8, 128], BF16)
    esink = sb("esink", [128, 8], F32)
    esr = sb("esr", [128, 8, 128], BF16)
    stat = sb("stat", [128, 64], F32)
    psF = es.enter_context(nc.psum_tensor("psF", [128, 6, 512], F32))
    psB = es.enter_context(nc.psum_tensor("psB", [128, 2, 1024], BF16))

    csem = {e: [es.enter_context(nc.semaphore(f"c_{e}{i}")) for i in range(Sched.NROT)] for e in Sched.COMPUTE}
    dsem = {q: [es.enter_context(nc.semaphore(f"d_{q}{i}")) for i in range(Sched.KDMA)] for q in Sched.DMAQ}

    C_G1, C_GM, C_G2, C_GOA, C_GOB, C_BS, C_SINK = 0, 16, 32, 48, 56, 64, 72
    ST_SS, ST_T, ST_R = 0, 1, 2
    ST_RA, ST_RB = 8, 16
    ST_A, ST_TT = 40, 44
    ST_N = 24

    def stage(n):
        if n > kstop:
            raise StopEmit()

    def emit_all(S, piece_seq, record):
        try:
            emit_all_(S, piece_seq, record)
        except StopEmit:
            pass

    def emit_all_(S, piece_seq, record):
        PB = RR(5)
        wstate = dict(i=0, loaded=0, uses={})

        def src_aps(key):
            kind = key[0]
            if kind == "gu":
                _, L, j = key
                return [(Wd[nm, L].rearrange("(k p) f -> p k f", p=128)[:, :, j * 256:(j + 1) * 256], gi * 4096, "p (k f) -> p k f", dict(k=16))
                        for gi, nm in enumerate(("g", "u"))]
            if kind == "d":
                _, L, g = key
                return [(Wd["d", L][g * 512:(g + 1) * 512, :].rearrange("(c p) d -> p c d", p=128), 0, "p (c d) -> p c d", dict(c=4))]
            if kind == "in":
                c = key[1]
                return [(w_in_d.rearrange("(k p) f -> p k f", p=128)[:, :, c * 512:(c + 1) * 512], 0, "p (k f) -> p k f", dict(k=16))]
            i = key[1]
            return [(w_out_d[i * 512:(i + 1) * 512, :].rearrange("(c p) d -> p c d", p=128), 0, "p (c d) -> p c d", dict(c=4))]

        def scr_ap(key):
            kind = key[0]
            if kind == "gu":
                return scr["gu", key[1]][key[2]].rearrange("p a k f -> p (a k f)")
            if kind == "d":
                return scr["d", key[1]][key[2]].rearrange("p c d -> p (c d)")
            if kind == "in":
                return scr["in"][key[1]].rearrange("p k f -> p (k f)")
            return scr["out"][key[1]].rearrange("p c d -> p (c d)")

        def wuse(key):
            i = wstate["i"]
            wstate["i"] += 1
            if record:
                piece_seq.append(key)
                return i % NW
            assert piece_seq[i] == key, (i, piece_seq[i], key)
            while wstate["loaded"] < min(len(piece_seq), i + NW):
                li = wstate["loaded"]
                k2 = piece_seq[li]
                slot = li % NW
                nuse = wstate["uses"].get(k2, 0)
                wstate["uses"][k2] = nuse + 1
                n_direct = 2 if k2[0] == "d" else 1
                if nuse < n_direct:
                    parts = src_aps(k2)
                    n = 8192 // len(parts)
                    for pi_, (src, off, pat, kw) in enumerate(parts):
                        dst = wring[:, slot, off:off + n].rearrange(pat, **kw)
                        wat = (("w", slot, pi_),) if len(parts) == 2 else (("w", slot, 0), ("w", slot, 1))
                        S.add("pool", lambda e, s=src, d=dst: e.dma_start(out=d, in_=s), writes=wat)
                    if nuse == n_direct - 1:
                        S.add("sp", lambda e, d=scr_ap(k2), sl=slot: e.dma_start(out=d, in_=wring[:, sl, :]),
                              reads=(("w", slot, 0), ("w", slot, 1)), writes=(("scr", k2),))
                else:
                    S.add("sp", lambda e, s=scr_ap(k2), sl=slot: e.dma_start(out=wring[:, sl, :], in_=s),
                          reads=(("scr", k2),), writes=(("w", slot, 0), ("w", slot, 1)))
                wstate["loaded"] += 1
            return i % NW

        def rstd_from_ss(ss_ap, nfeat, out_ap, ss_atoms, out_atom, n=1, niter=2):
            if not isinstance(ss_atoms[0], tuple):
                ss_atoms = (ss_atoms,)
            av = stat[:, ST_A:ST_A + n]
            tv = stat[:, ST_TT:ST_TT + n]
            A_, T_ = ("st", "rs_a"), ("st", "rs_t")
            S.add("dve", lambda e: e.tensor_scalar(out=av, in0=ss_ap, scalar1=1.0 / nfeat, scalar2=EPS,
                                                   op0=ALU.mult, op1=ALU.add), reads=tuple(ss_atoms), writes=(A_,))
            S.add("dve", lambda e: e.tensor_copy(out=tv, in_=av.bitcast(I32)), reads=(A_,), writes=(T_,))
            S.add("dve", lambda e: e.tensor_scalar(out=out_ap.bitcast(I32), in0=tv, scalar1=-0.5, scalar2=float(0x5f3759df),
                                                   op0=ALU.mult, op1=ALU.add), reads=(T_,), writes=(out_atom,))
            for _ in range(niter):
                if n == 1:
                    S.add("dve", lambda e: e.scalar_tensor_tensor(out=tv, in0=out_ap, scalar=av, in1=out_ap,
                                                                  op0=ALU.mult, op1=ALU.mult),
                          reads=(out_atom, A_), writes=(T_,))
                else:
                    S.add("dve", lambda e: e.tensor_tensor(out=tv, in0=out_ap, in1=out_ap, op=ALU.mult), reads=(out_atom,), writes=(T_,))
                    S.add("dve", lambda e: e.tensor_tensor(out=tv, in0=tv, in1=av, op=ALU.mult), reads=(T_, A_), writes=(T_,))
                S.add("dve", lambda e: e.tensor_scalar(out=tv, in0=tv, scalar1=-0.5, scalar2=1.5, op0=ALU.mult, op1=ALU.add),
                      reads=(T_,), writes=(T_,))
                S.add("dve", lambda e: e.tensor_tensor(out=out_ap, in0=out_ap, in1=tv, op=ALU.mult), reads=(out_atom, T_), writes=(out_atom,))

        def sumsq(src_ap, src_atoms, junk_ap, junk_atom, ss_ap, ss_atom):
            S.add("act", lambda e: e.activation(out=junk_ap, in_=src_ap, func=AF.Square, accum_out=ss_ap),
                  reads=src_atoms, writes=(junk_atom, ss_atom))

        def norm_stats(j, hs):
            sumsq(hbuf[:, hs, :], (("h", hs),), xn_bf[:, j % 2, :], ("xnbf", j % 2),
                  stat[:, ST_N + j:ST_N + j + 1], ("st", "nss", j))

        def norm_tile(slots, gcol, stats_done=False, split=False):
            nbk = len(slots)
            groups = [list(range(nbk))]
            if split and nbk == 4 and not stats_done:
                groups = [[0, 1], [2, 3]]

            def stA(j, hs, ratom):
                hap = hbuf[:, hs, :]
                xb = j % 2
                rr = stat[:, ST_N + 8 + j:ST_N + 9 + j]
                S.add("act", lambda e: e.activation(out=xn_bf[:, xb, :], in_=hap, func=AF.Copy, scale=rr),
                      reads=(("h", hs), ratom), writes=(("xnbf", xb),))

            def stB(j):
                xb = j % 2
                xa = ("xnbf", xb)
                for half in range(2):
                    for c in range(8):
                        k = half * 8 + c
                        S.add("pe", lambda e, half=half, c=c, k=k, xb=xb: e.transpose(
                            out=psB[:, half, c * 128:(c + 1) * 128], in_=xn_bf[:, xb, k * 128:(k + 1) * 128], identity=ident[:, :]),
                            reads=(xa, ("ident",)), writes=(("psB", half),))
                    gsrc = cst[:, gcol + half * 8: gcol + half * 8 + 8].unsqueeze(2).to_broadcast([128, 8, 128])
                    S.add("dve", lambda e, half=half, gsrc=gsrc, j=j: e.tensor_tensor(
                        out=xnT[:, half * 8:(half + 1) * 8, j * 128:(j + 1) * 128],
                        in0=psB[:, half, :].rearrange("p (c t) -> p c t", c=8), in1=gsrc, op=ALU.mult),
                        reads=(("psB", half), ("cst",)), writes=(("xnT", j),))

            for gi, grp in enumerate(groups):
                if not stats_done:
                    for j in grp:
                        norm_stats(j, slots[j])
                j0, n = grp[0], len(grp)
                ratom = ("st", "nr", gi)
                rstd_from_ss(stat[:, ST_N + j0:ST_N + j0 + n], D, stat[:, ST_N + 8 + j0:ST_N + 8 + j0 + n],
                             tuple(("st", "nss", j) for j in grp), ratom, n=n)
                for q in range(n + 1):
                    if q < n:
                        stA(grp[q], slots[grp[q]], ratom)
                    if q >= 1:
                        stB(grp[q - 1])

        def proj_tm(slot, lhs_fn, nk, hs, scale, extra_reads):
            wv = wring[:, slot, :].rearrange("p (c d) -> p c d", c=4)
            for half in range(2):
                banks = PB.alloc(2)
                for c in range(nk):
                    for q in range(2):
                        S.add("pe", lambda e, c=c, q=q, half=half, b=banks[q]: e.matmul(
                            psF[:, b, :], lhsT=lhs_fn(c), rhs=wv[:, c, half * 1024 + q * 512: half * 1024 + (q + 1) * 512],
                            start=(c == 0), stop=(c == nk - 1)),
                            reads=(("w", slot, 0), ("w", slot, 1)) + extra_reads, writes=(("ps", banks[q]),))
                pin = psF[:, banks[0]:banks[0] + 2, :].rearrange("p b n -> p (b n)")
                hap = hbuf[:, hs, half * 1024:(half + 1) * 1024]
                sc_reads = () if isinstance(scale, float) else (scale[1],)
                sc = scale if isinstance(scale, float) else scale[0]
                S.add("dve", lambda e, pin=pin, hap=hap, sc=sc: e.scalar_tensor_tensor(
                    out=hap, in0=pin, scalar=sc, in1=hap, op0=ALU.mult, op1=ALU.add),
                    reads=(("ps", banks[0]), ("ps", banks[1]), ("h", hs)) + sc_reads, writes=(("h", hs),))

        def ffn(L, tile_slots, last_hook=None):
            nbk = len(tile_slots)
            T = nbk * 128
            xatoms = tuple(("xnT", j) for j in range(nbk))

            def GU(g):
                buf = g % 2
                for pi in range(2):
                    slot = wuse(("gu", L, 2 * g + pi))
                    wv = wring[:, slot, :].rearrange("p (a k f) -> p a k f", a=2, k=16)
                    for cc in range(2):
                        c4 = pi * 2 + cc
                        banks = PB.alloc(2)
                        for a in range(2):
                            for k in range(16):
                                S.add("pe", lambda e, a=a, k=k, cc=cc, b=banks[a], wv=wv: e.matmul(
                                    psF[:, b, :T], lhsT=wv[:, a, k, cc * 128:(cc + 1) * 128], rhs=xnT[:, k, :T],
                                    start=(k == 0), stop=(k == 15)),
                                    reads=(("w", slot, 0), ("w", slot, 1)) + xatoms, writes=(("ps", banks[a]),))
                        si = c4 % 2
                        S.add("act", lambda e, b=banks[0], si=si: e.activation(out=sgt[:, si, :T], in_=psF[:, b, :T], func=AF.Silu),
                              reads=(("ps", banks[0]),), writes=(("sg", si),))
                        S.add("dve", lambda e, b=banks[1], si=si, buf=buf, c4=c4: e.tensor_tensor(
                            out=actT[:, buf, c4, :T], in0=psF[:, b, :T], in1=sgt[:, si, :T], op=ALU.mult),
                            reads=(("ps", banks[1]), ("sg", si)), writes=(("act", buf, c4),))

            def DN(g):
                buf = g % 2
                slot = wuse(("d", L, g))
                for j, hs in enumerate(tile_slots):
                    proj_tm(slot, lambda c, j=j, buf=buf: actT[:, buf, c, j * 128:(j + 1) * 128], 4, hs, 0.5,
                            tuple(("act", buf, c) for c in range(4)))
                    if g == NGRP - 1 and last_hook is not None:
                        last_hook(j, hs)

            for g in range(NGRP + 1):
                if g < NGRP:
                    GU(g)
                if g >= 1:
                    DN(g - 1)

        S.add("pool", lambda e: e.dma_start(out=cst[:, :], in_=cst_d), writes=(("cst",),))
        S.add("pool", lambda e: e.dma_start(out=gbc[:, :], in_=gbc_d), writes=(("gbc",),))
        for hh in range(4):
            S.add("pool", lambda e, hh=hh: e.dma_start(out=masks[:, :, hh, :], in_=masks_d), writes=(("masks",),))
        S.add("pool", lambda e: e.dma_start(out=ident[:, :], in_=ident_d), writes=(("ident",),))
        S.add("pool", lambda e: e.dma_start(out=wsT[:, :, :], in_=wsT_d), writes=(("wsT",),))
        S.add("dve", lambda e: e.memset(ones[:, :], 1.0), writes=(("ones",),))
        S.add("act", lambda e: e.activation(out=esink[:, :], in_=cst[:, C_SINK:C_SINK + 8], func=AF.Exp),
              reads=(("cst",),), writes=(("esink",),))
        S.add("dve", lambda e: e.tensor_scalar(out=esr[:, :, :], in0=esink[:, :].unsqueeze(2).to_broadcast([128, 8, 128]),
                                               scalar1=1.0 / 128.0, scalar2=None, op0=ALU.mult),
              reads=(("esink",),), writes=(("esr",),))

        hfree = list(range(NH))
        qfree = list(range(NQ))
        kvfree = list(range(NKV))
        hslot, qslot, kvslot = {}, {}, {}

        def phaseA(t):
            tb = cfg["a_tiles"][t]
            nbk = len(tb)
            for j, b in enumerate(tb):
                hs_ = hslot[b] = hfree.pop(0)
                kvslot[b] = kvfree.pop(0)
                if blocks[b]["kind"] == "own":
                    qslot[b] = qfree.pop(0)
                S.add("sp", lambda e, b=b, hs_=hs_: e.dma_start(out=hbuf[:, hs_, :], in_=xs[b]), writes=(("h", hs_),))
                S.add("pool", lambda e, b=b, j=j: e.dma_start(out=ropeT[:, j, :], in_=rope_d[b]), writes=(("rope", j),))
            slots = [hslot[b] for b in tb]
            stage(2)
            norm_tile(slots, C_G1, split=True)
            stage(3)
            ffn(1, slots, last_hook=norm_stats)
            stage(4)
            norm_tile(slots, C_GM, stats_done=True)
            stage(4.1)
            dq = Defer()

            def g_start(j):
                ss = stat[:, ST_SS:ST_SS + 1]
                rr = stat[:, ST_R:ST_R + 1]
                vga = (("vg", j, 0), ("vg", j, 1))
                sumsq(vg_bf[:, j, :], vga, junk[:, :], ("junk",), ss, ("st", "ss"))
                rstd_from_ss(ss, 1024, rr, ("st", "ss"), ("st", "r"))
                S.add("dve", lambda e, j=j: e.scalar_tensor_tensor(
                    out=vg_bf[:, j, :], in0=vg_bf[:, j, :], scalar=stat[:, ST_R:ST_R + 1], in1=gbc[:, 2048:3072],
                    op0=ALU.mult, op1=ALU.mult),
                    reads=vga + (("st", "r"), ("gbc",)), writes=vga)

            def g_mix(j):
                ss = stat[:, ST_SS:ST_SS + 1]
                vga = (("vg", j, 0), ("vg", j, 1))
                ua = (("u", j, 0), ("u", j, 1))
                ab = j % 2
                mb = PB.alloc(2)
                for g in range(8):
                    S.add("pe", lambda e, g=g, j=j, bk=mb[g // 4]: e.matmul(
                        psF[:, bk, (g % 4) * 128:(g % 4 + 1) * 128], lhsT=wsT[:, g, :],
                        rhs=vg_bf[:, j, g * 128:(g + 1) * 128], start=True, stop=True),
                        reads=vga + (("wsT",),), writes=(("ps", mb[g // 4]),))
                for g in range(8):
                    S.add("dve", lambda e, g=g, j=j, bk=mb[g // 4], ab=ab: e.scalar_tensor_tensor(
                        out=a_bf[:, ab, g * 128:(g + 1) * 128], in0=psF[:, bk, (g % 4) * 128:(g % 4 + 1) * 128],
                        scalar=cst[:, C_BS + g:C_BS + g + 1], in1=u_bf[:, j, g * 128:(g + 1) * 128],
                        op0=ALU.add, op1=ALU.mult),
                        reads=ua + (("ps", mb[g // 4]), ("cst",)), writes=(("abf", ab),))
                dq.add(1, lambda: g_stat(j))

            def g_stat(j):
                ss = stat[:, ST_SS:ST_SS + 1]
                ab = j % 2
                sumsq(a_bf[:, ab, :], (("abf", ab),), junk[:, :], ("junk",), ss, ("st", "ss"))
                rstd_from_ss(ss, 1024, stat[:, ST_RA + j:ST_RA + j + 1], ("st", "ss"), ("st", "ra", j))
                dq.add(1, lambda: g_tr(j))

            def g_tr(j):
                ab = j % 2
                for g in range(8):
                    S.add("pe", lambda e, g=g, ab=ab: e.transpose(
                        out=psB[:, 1, g * 128:(g + 1) * 128], in_=a_bf[:, ab, g * 128:(g + 1) * 128], identity=ident[:, :]),
                        reads=(("abf", ab), ("ident",)), writes=(("psB", 1),))
                gsrc = cst[:, C_GOA:C_GOA + 8].unsqueeze(2).to_broadcast([128, 8, 128])
                S.add("dve", lambda e, j=j, gsrc=gsrc: e.tensor_tensor(
                    out=mT[:, :, j * 128:(j + 1) * 128], in0=psB[:, 1, :].rearrange("p (c t) -> p c t", c=8),
                    in1=gsrc, op=ALU.mult),
                    reads=(("psB", 1), ("cst",)), writes=(("mT", j),))

            def q_tr(qi, nh, dst, watom):
                for h in range(nh):
                    S.add("pe", lambda e, h=h, qi=qi: e.transpose(
                        out=psB[:, 0, h * 128:(h + 1) * 128], in_=qb[:, qi, h * 128:(h + 1) * 128], identity=ident[:, :]),
                        reads=(("qb", qi, 0), ("qb", qi, 1), ("qb", qi, 2), ("ident",)), writes=(("psB", 0),))
                S.add("act", lambda e, dst=dst, nh=nh: e.activation(
                    out=dst, in_=psB[:, 0, 0:nh * 128].rearrange("p (h t) -> p h t", h=nh), func=AF.Copy),
                    reads=(("psB", 0),), writes=(watom,))

            ui = 0
            for c in (2, 3, 0, 1, 4, 5, 6):
                if c == 4:
                    stage(4.5)
                if c == 6:
                    stage(4.7)
                slot = wuse(("in", c))
                wv = wring[:, slot, :].rearrange("p (k f) -> p k f", k=16)
                for j, b in enumerate(tb):
                    own = blocks[b]["kind"] == "own"
                    if not own and c < 6:
                        continue
                    bank = PB.alloc(1)[0]
                    for k in range(16):
                        S.add("pe", lambda e, k=k, j=j, bank=bank, wv=wv: e.matmul(
                            psF[:, bank, :], lhsT=xnT[:, k, j * 128:(j + 1) * 128], rhs=wv[:, k, :],
                            start=(k == 0), stop=(k == 15)),
                            reads=(("w", slot, 0), ("w", slot, 1), ("xnT", j)), writes=(("ps", bank),))
                    dq.tick()
                    ui += 1
                    zp = psF[:, bank, :]
                    if c < 2:
                        S.add("act", lambda e, zp=zp, j=j, c=c: e.activation(
                            out=u_bf[:, j, c * 512:(c + 1) * 512], in_=zp, func=AF.Gelu_apprx_tanh),
                            reads=(("ps", bank),), writes=(("u", j, c),))
                        if c == 1 and own and kstop >= 4.3:
                            dq.add(1, lambda j=j: g_mix(j))
                    elif c < 4:
                        S.add("act", lambda e, zp=zp, j=j, c=c: e.activation(
                            out=vg_bf[:, j, (c - 2) * 512:(c - 1) * 512], in_=zp, func=AF.Gelu_apprx_tanh),
                            reads=(("ps", bank),), writes=(("vg", j, c - 2),))
                        if c == 3 and own and kstop >= 4.3:
                            g_start(j)
                    else:
                        nh = 4 if c < 6 else 2
                        qi = ui % 2
                        z3 = zp.rearrange("p (h d) -> p h d", h=4)
                        qb3 = qb[:, qi, :].rearrange("p (h d) -> p h d", h=4)
                        z3t = zp.rearrange("p (h d) -> p d h", h=4)
                        qb3t = qb[:, qi, :].rearrange("p (h d) -> p d h", h=4)
                        cs1 = ropeT[:, j, 0:32].unsqueeze(2).to_broadcast([128, 32, nh])
                        cs2 = ropeT[:, j, 32:64].unsqueeze(2).to_broadcast([128, 32, nh])
                        S.add("dve", lambda e, z3t=z3t, cs1=cs1, nh=nh: e.tensor_tensor(
                            out=rp[:, 0, :, 0:nh], in0=z3t[:, 0:32, 0:nh], in1=cs1, op=ALU.mult),
                            reads=(("ps", bank), ("rope", j)), writes=(("rp", 0),))
                        S.add("dve", lambda e, z3t=z3t, cs2=cs2, nh=nh: e.tensor_tensor(
                            out=rp[:, 1, :, 0:nh], in0=z3t[:, 0:32, 0:nh], in1=cs2, op=ALU.mult),
                            reads=(("ps", bank), ("rope", j)), writes=(("rp", 1),))
                        S.add("act", lambda e, z3=z3, qb3=qb3, nh=nh: e.activation(
                            out=qb3[:, 0:nh, 32:128], in_=z3[:, 0:nh, 32:128], func=AF.Copy),
                            reads=(("ps", bank),), writes=(("qb", qi, 2),))
                        if c == 6:
                            S.add("act", lambda e, zp=zp, ks_=kvslot[b]: e.activation(
                                out=vv[:, ks_, :], in_=zp[:, 256:512], func=AF.Copy),
                                reads=(("ps", bank),), writes=(("kv", kvslot[b], "v"),))
                        S.add("dve", lambda e, qb3t=qb3t, nh=nh: e.tensor_tensor(
                            out=qb3t[:, 0:16, 0:nh], in0=rp[:, 0, 0:16, 0:nh], in1=rp[:, 0, 16:32, 0:nh], op=ALU.subtract),
                            reads=(("rp", 0),), writes=(("qb", qi, 0),))
                        S.add("dve", lambda e, qb3t=qb3t, nh=nh: e.tensor_tensor(
                            out=qb3t[:, 16:32, 0:nh], in0=rp[:, 1, 0:16, 0:nh], in1=rp[:, 1, 16:32, 0:nh], op=ALU.add),
                            reads=(("rp", 1),), writes=(("qb", qi, 1),))
                        if c < 6:
                            dst = qT[:, qslot[b], (c - 4) * 4:(c - 3) * 4, :]
                            watom = ("q", qslot[b], c - 4)
                        else:
                            dst = kT[:, kvslot[b], :, :]
                            watom = ("kv", kvslot[b], "k")
                        dq.add(1, lambda qi=qi, nh=nh, dst=dst, watom=watom: q_tr(qi, nh, dst, watom))
            dq.flush()
            stage(5)
            for i in range(2):
                slot = wuse(("out", i))
                for j, b in enumerate(tb):
                    if blocks[b]["kind"] != "own":
                        continue
                    proj_tm(slot, lambda c, j=j, i=i: mT[:, i * 4 + c, j * 128:(j + 1) * 128], 4, hslot[b],
                            (stat[:, ST_RA + j:ST_RA + j + 1], ("st", "ra", j)), (("mT", j),))
            for b in tb:
                if blocks[b]["kind"] != "own":
                    hfree.append(hslot.pop(b))

        def phaseB(t, si):
            tb = cfg["b_tiles"][t]
            if not tb:
                return
            scale = 1.0 / np.sqrt(128.0)
            stage(6)
            dq = Defer()

            def att_P(j, kvh, nbrs, pb):
                bo_bank = PB.alloc(1)[0]
                bd_bank = PB.alloc(1)[0]
                for kb, (nb_, mid) in enumerate(nbrs):
                    ks = kvslot_snapshot[nb_]
                    S.add("pe", lambda e, ks=ks, kb=kb, kvh=kvh, bank=bo_bank, pb=pb: e.matmul(
                        psF[:, bank, :], lhsT=vv[:, ks, kvh * 128:(kvh + 1) * 128], rhs=PT[:, pb, kb, :],
                        start=(kb == 0), stop=(kb == 2)),
                        reads=(("kv", ks, "v"), ("PT", pb, kb)), writes=(("ps", bo_bank),))
                for kb in range(3):
                    S.add("pe", lambda e, kb=kb, bank=bd_bank, pb=pb: e.matmul(
                        psF[:, bank, :], lhsT=ones[:, :], rhs=PT[:, pb, kb, :], start=(kb == 0), stop=False),
                        reads=(("ones",), ("PT", pb, kb)), writes=(("ps", bd_bank),))
                S.add("pe", lambda e, bank=bd_bank, kvh=kvh: e.matmul(
                    psF[:, bank, :], lhsT=ones[:, :], rhs=esr[:, kvh * 4:(kvh + 1) * 4, :], start=False, stop=True),
                    reads=(("ones",), ("esr",)), writes=(("ps", bd_bank),))
                S.add("act", lambda e, bank=bd_bank: e.activation(out=den[:, :], in_=psF[:, bank, :], func=AF.Ln),
                      reads=(("ps", bd_bank),), writes=(("den",),))
                S.add("act", lambda e: e.activation(out=den[:, :], in_=den[:, :], func=AF.Exp, scale=-1.0),
                      reads=(("den",),), writes=(("den",),))
                S.add("dve", lambda e, bank=bo_bank: e.tensor_tensor(out=bo[:, :], in0=psF[:, bank, :], in1=den[:, :], op=ALU.mult),
                      reads=(("ps", bo_bank), ("den",)), writes=(("bo",),))
                S.add("dve", lambda e: e.tensor_tensor(out=sq[:, :], in0=bo[:, :], in1=bo[:, :], op=ALU.mult),
                      reads=(("bo",),), writes=(("sq",),))
                gob = cst[:, C_GOB + kvh * 4:C_GOB + kvh * 4 + 4].unsqueeze(2).to_broadcast([128, 4, 128])
                S.add("dve", lambda e, j=j, kvh=kvh, gob=gob: e.tensor_tensor(
                    out=mT[:, kvh * 4:(kvh + 1) * 4, j * 128:(j + 1) * 128],
                    in0=bo[:, :].rearrange("p (h t) -> p h t", h=4), in1=gob, op=ALU.mult),
                    reads=(("bo",), ("cst",)), writes=(("mT", j),))
                dq.add(1, lambda: att_Q(j, kvh))

            def att_Q(j, kvh):
                for h in range(4):
                    S.add("pe", lambda e, h=h, kvh=kvh: e.matmul(
                        psF[:, 5, 0:1], lhsT=sq[:, h * 128:(h + 1) * 128], rhs=ones[:, 0:1],
                        start=(kvh == 0 and h == 0), stop=(kvh == 1 and h == 3)),
                        reads=(("sq",), ("ones",)), writes=(("ps", 5),))
                if kvh == 1:
                    ssb = stat[:, ST_SS:ST_SS + 1]
                    S.add("act", lambda e: e.activation(out=stat[:, ST_SS:ST_SS + 1], in_=psF[:, 5, 0:1], func=AF.Copy),
                          reads=(("ps", 5),), writes=(("st", "ss"),))
                    rstd_from_ss(ssb, 1024, stat[:, ST_RB + j:ST_RB + j + 1], ("st", "ss"), ("st", "rb", j))

            kvslot_snapshot = dict(kvslot)
            ui = 0
            for j, b in enumerate(tb):
                bl = blocks[b]
                nbrs = [(bl["left"], bl["lm"]), (b, None), (bl["right"], bl["rm"])]
                qs = qslot[b]
                for kvh in range(2):
                    pb = ui % 2
                    ui += 1
                    for kb, (nb_, mid) in enumerate(nbrs):
                        bank = PB.alloc(1)[0]
                        ks = kvslot[nb_]
                        S.add("pe", lambda e, bank=bank, ks=ks, kvh=kvh, qs=qs, mid=mid: e.matmul(
                            psF[:, bank, :], lhsT=kT[:, ks, kvh, :], rhs=qT[:, qs, kvh * 4:(kvh + 1) * 4, :],
                            start=True, stop=(mid is None)),
                            reads=(("kv", ks, "k"), ("q", qs, kvh)), writes=(("ps", bank),))
                        if mid is not None:
                            S.add("pe", lambda e, bank=bank, mid=mid: e.matmul(
                                psF[:, bank, :], lhsT=ident[:, :],
                                rhs=masks[:, mid, :, :],
                                start=False, stop=True),
                                reads=(("ident",), ("masks",)), writes=(("ps", bank),))
                        S.add("act", lambda e, bank=bank, kb=kb, pb=pb: e.activation(
                            out=PT[:, pb, kb, :], in_=psF[:, bank, :], func=AF.Exp, scale=float(scale)),
                            reads=(("ps", bank),), writes=(("PT", pb, kb),))
                    dq.tick()
                    dq.add(1, lambda j=j, kvh=kvh, nbrs=nbrs, pb=pb: att_P(j, kvh, nbrs, pb))
            dq.flush()
            stage(7)
            for i in range(2):
                slot = wuse(("out", 2 + i))
                for j, b in enumerate(tb):
                    proj_tm(slot, lambda c, j=j, i=i: mT[:, i * 4 + c, j * 128:(j + 1) * 128], 4, hslot[b],
                            (stat[:, ST_RB + j:ST_RB + j + 1], ("st", "rb", j)), (("mT", j),))
                    if i == 1:
                        norm_stats(j, hslot[b])
            stage(8)
            norm_tile([hslot[b] for b in tb], C_G2, stats_done=True)
            ffn(2, [hslot[b] for b in tb], last_hook=norm_stats)
            stage(9)
            rstd_from_ss(stat[:, ST_N:ST_N + len(tb)], D, stat[:, ST_N + 8:ST_N + 8 + len(tb)],
                         tuple(("st", "nss", j) for j in range(len(tb))), ("st", "nr", 0), n=len(tb), niter=3)
            for j, b in enumerate(tb):
                hs = hslot[b]
                hap = hbuf[:, hs, :]
                rr = stat[:, ST_N + 8 + j:ST_N + 9 + j]
                S.add("dve", lambda e, hap=hap, rr=rr: e.scalar_tensor_tensor(
                    out=hap, in0=hap, scalar=rr, in1=gbc[:, 0:2048], op0=ALU.mult, op1=ALU.mult),
                    reads=(("h", hs), ("st", "nr", 0), ("gbc",)), writes=(("h", hs),))
                orow = blocks[b]["orow"]
                S.add("sp", lambda e, hap=hap, orow=orow: e.dma_start(out=y_d[orow * 128:(orow + 1) * 128, :], in_=hap),
                      reads=(("h", hs),), writes=(("y", orow),))
                hfree.append(hslot.pop(b))
                qfree.append(qslot.pop(b))
            for b_ in [b_ for b_ in list(kvslot) if kv_last.get(b_, -1) == si]:
                kvfree.append(kvslot.pop(b_))

        for si, (k, t) in enumerate(steps):
            if k == "A":
                phaseA(t)
            else:
                phaseB(t, si)

    piece_seq = []
    emit_all(Sched(dry=True), piece_seq, True)
    S = Sched()
    emit_all(S, piece_seq, False)
    S.analyze()
    with nc.Block() as block:
        S.emit(nc, block, csem, dsem)
    es.close()
    return nc, S.stats


def rope_table(pos):
    inv = (np.float32(ROPE_THETA) ** (-np.arange(0, 32, 2, dtype=np.float32) / np.float32(32))).astype(np.float32)
    ang = pos.astype(np.float32)[:, None] * inv[None, :]
    c = np.cos(ang).astype(np.float32)
    s = np.sin(ang).astype(np.float32)
    return np.concatenate([c, s, s, c], axis=1)


def band_masks(l_p1, r_pl, l_s0, r_sl):
    j = np.arange(128)[:, None]
    i = np.arange(128)[None, :]
    Lm = np.where(j >= i, 0.0, MASKV).astype(np.float32)
    Rm = np.where(j <= i, 0.0, MASKV).astype(np.float32)
    full = np.full((128, 128), MASKV, np.float32)
    m = np.stack([Lm, Rm, Lm if l_p1 else full, Rm if r_pl else full, Lm if l_s0 else full, Rm if r_sl else full], axis=1)
    return np.ascontiguousarray(m)


def common_inputs(ffn1_norm, ffn1_w_gate, ffn1_w_up, ffn1_w_down, mix_norm, w_in, gmlp_v_norm, gmlp_w_s, gmlp_b_s,
                  attn_sink, out_norm_gmlp, out_norm_attn, w_out, ffn2_norm, ffn2_w_gate, ffn2_w_up, ffn2_w_down,
                  final_norm):
    f = lambda a: np.ascontiguousarray(np.asarray(a, dtype=np.float32))
    cst = np.zeros((128, 80), np.float32)
    cst[:, 0:16] = f(ffn1_norm)[0].reshape(16, 128).T
    cst[:, 16:32] = f(mix_norm)[0].reshape(16, 128).T
    cst[:, 32:48] = f(ffn2_norm)[0].reshape(16, 128).T
    cst[:, 48:56] = f(out_norm_gmlp)[0].reshape(8, 128).T
    cst[:, 56:64] = f(out_norm_attn)[0].reshape(8, 128).T
    cst[:, 64:72] = f(gmlp_b_s)[0].T
    cst[:, 72:80] = np.broadcast_to(f(attn_sink)[0][None, :], (128, 8))
    gbc = np.empty((128, 3072), np.float32)
    gbc[:, 0:2048] = np.broadcast_to(f(final_norm)[None, :], (128, 2048))
    gbc[:, 2048:3072] = np.broadcast_to(f(gmlp_v_norm)[0][None, :], (128, 1024))
    wsT = np.ascontiguousarray(np.transpose(f(gmlp_w_s)[0], (2, 0, 1)))
    return {
        "cst": cst, "gbc": gbc, "ident": np.eye(128, dtype=np.float32), "wsT": wsT,
        "wg1": f(ffn1_w_gate)[0], "wu1": f(ffn1_w_up)[0], "wd1": f(ffn1_w_down)[0],
        "wg2": f(ffn2_w_gate)[0], "wu2": f(ffn2_w_up)[0], "wd2": f(ffn2_w_down)[0],
        "w_in": f(w_in)[0], "w_out": f(w_out)[0],
    }


_CACHE = {}


def kernel(x_prompt, x_sample, **w):
    xp = np.asarray(x_prompt, dtype=np.float32)
    xsm = np.asarray(x_sample, dtype=np.float32)
    NPO, NSO = 16, 8
    cfg = make_cfg(NPO, NSO)
    if "nc" not in _CACHE:
        _CACHE["nc"] = build_program(cfg)[0]
    nc = _CACHE["nc"]
    com = common_inputs(**w)
    SEQ = xp.shape[1]
    npb = SEQ // 128
    in_maps = []
    zero = np.zeros((128, D), np.float32)
    for i in range(8):
        blks, ropes = [], []

        def addp(gb):
            if 0 <= gb < npb:
                blks.append(xp[0, gb * 128:(gb + 1) * 128])
            else:
                blks.append(zero)
            ropes.append(rope_table(np.arange(gb * 128, (gb + 1) * 128)))
        sb_, hf = i // 2, i % 2
        halo = NSO if hf == 0 else NSO - 1
        addp(NPO * i - 1)
        addp(NPO * i + NPO)
        addp(NPO * i)
        addp(NPO * i + 1)
        blks.append(xsm[sb_, halo * 128:(halo + 1) * 128])
        ropes.append(rope_table(np.arange(halo * 128, (halo + 1) * 128)))
        for j in range(2, NPO):
            addp(NPO * i + j)
        for sblk in [NSO * hf + j for j in range(NSO)]:
            blks.append(xsm[sb_, sblk * 128:(sblk + 1) * 128])
            ropes.append(rope_table(np.arange(sblk * 128, (sblk + 1) * 128)))
        m = dict(com)
        m["xs"] = np.ascontiguousarray(np.stack(blks, 0))
        m["rope"] = np.ascontiguousarray(np.stack(ropes, 0))
        m["masks"] = band_masks(l_p1=(i > 0), r_pl=(i < 7), l_s0=(hf == 1), r_sl=(hf == 0))
        in_maps.append(m)
    res = run_bass_kernel_spmd(nc, in_maps, core_ids=list(range(8)))
    y_p = np.empty_like(xp)
    y_s = np.empty_like(xsm)
    for i in range(8):
        y = res.results[i]["y"]
        y_p[0, 2048 * i:2048 * (i + 1)] = y[:2048]
        y_s[i // 2, 1024 * (i % 2):1024 * (i % 2 + 1)] = y[2048:3072]
    return (y_p, y_s)
```
